# Optimizing a Trainium2 kernel written in Bass

```python
import math
import jax, jax.numpy as jnp
from jax import lax
import numpy as np

D_MODEL = 1024
BATCH = 32
SEQ = 2048
DEPTH = 2

GRID_W = 64
CTX_LEN = 256
EPS = 1e-6

MLA_HEADS = 4
MLA_Q_LORA = 256
MLA_KV_LORA = 128
MLA_NOPE = 128
MLA_ROPE = 64
MLA_V = 128
MLA_WIDTH = MLA_HEADS * MLA_V
ROPE_BASE = 10000.0
ATTN_BLOCK = 128

HG_HEADS = 4
HG_K = 128
HG_V = 64
HG_WIDTH = HG_HEADS * HG_V
HG_CHUNK = 64

FN_GROUPS = 4
FN_GROUP_DIM = 64
FN_WIDTH = FN_GROUPS * FN_GROUP_DIM

MIX_WIDTH = MLA_WIDTH + HG_WIDTH + FN_WIDTH
FFN_HIDDEN = -(-8 * D_MODEL // (3 * 256)) * 256

O_CQ = 0
O_CKV = O_CQ + MLA_Q_LORA
O_KR = O_CKV + MLA_KV_LORA
O_HQ = O_KR + MLA_ROPE
O_HFF = O_HQ + HG_HEADS * HG_K
O_HFB = O_HFF + HG_HEADS * HG_K
O_HI = O_HFB + HG_HEADS * HG_K
O_HG = O_HI + HG_WIDTH
O_FN = O_HG + HG_WIDTH
IN_WIDTH = O_FN + FN_WIDTH

kernel_name = 'hybrid_mla_hgrn2_fnet_dit'


def rms_norm(x, g):
    xf = x.astype(jnp.float32)
    y = xf * lax.rsqrt(jnp.mean(xf * xf, axis=-1, keepdims=True) + EPS)
    return (y * g.astype(jnp.float32)).astype(x.dtype)


def modulate(x, g, shift, scale):
    return rms_norm(x, g) * (1 + scale) + shift


def axial_rope(n_tokens, dtype):
    rows = n_tokens // GRID_W
    row_pos = jnp.repeat(jnp.arange(rows, dtype=jnp.float32), GRID_W)
    col_pos = jnp.tile(jnp.arange(GRID_W, dtype=jnp.float32), rows)
    axis_dim = MLA_ROPE // 2
    inv_freq = ROPE_BASE ** (-jnp.arange(0, axis_dim, 2, dtype=jnp.float32) / axis_dim)
    ang_r = row_pos[:, None] * inv_freq
    ang_c = col_pos[:, None] * inv_freq
    ang = jnp.concatenate([ang_r, ang_r, ang_c, ang_c], axis=-1)
    return (jnp.cos(ang).astype(dtype)[None, :, None, :], jnp.sin(ang).astype(dtype)[None, :, None, :])


def apply_rope(x, cos, sin):
    r1, r2, c1, c2 = jnp.split(x, 4, axis=-1)
    rot = jnp.concatenate([-r2, r1, -c2, c1], axis=-1)
    return x * cos + rot * sin


def mla_queries(p, q_norm_g, w_uq, rope):
    b, t, _ = p.shape
    cq = rms_norm(p[..., O_CQ:O_CKV], q_norm_g)
    q = (cq @ w_uq).reshape(b, t, MLA_HEADS, MLA_NOPE + MLA_ROPE)
    q_nope, q_pe = q[..., :MLA_NOPE], q[..., MLA_NOPE:]
    if rope is not None:
        q_pe = apply_rope(q_pe, rope[0], rope[1])
    return jnp.concatenate([q_nope, q_pe], axis=-1)


def mla_keys_values(p, kv_norm_g, w_ukv, rope):
    b, t, _ = p.shape
    ckv = rms_norm(p[..., O_CKV:O_KR], kv_norm_g)
    kv = (ckv @ w_ukv).reshape(b, t, MLA_HEADS, MLA_NOPE + MLA_V)
    k_nope, v = kv[..., :MLA_NOPE], kv[..., MLA_NOPE:]
    k_pe = p[..., O_KR:O_HQ][:, :, None, :]
    if rope is not None:
        k_pe = apply_rope(k_pe, rope[0], rope[1])
    k = jnp.concatenate([k_nope, jnp.broadcast_to(k_pe, (b, t, MLA_HEADS, MLA_ROPE))], axis=-1)
    return k, v


def block_attention(q, k, v):
    b, t, h, dk = q.shape
    nb = t // ATTN_BLOCK
    qb = q.reshape(b, nb, ATTN_BLOCK, h, dk).swapaxes(0, 1)
    scale = 1.0 / math.sqrt(dk)

    def one_block(qi):
        s = jnp.einsum('bqhd,bkhd->bhqk', qi, k).astype(jnp.float32) * scale
        pr = jax.nn.softmax(s, axis=-1).astype(v.dtype)
        return jnp.einsum('bhqk,bkhd->bqhd', pr, v)

    o = lax.map(one_block, qb)
    return o.swapaxes(0, 1).reshape(b, t, h * v.shape[-1])


def chunk_gated_scan(q, k, v, log_f, s0, with_output):
    b, t, h, kd = q.shape
    n = t // HG_CHUNK

    def to_chunks(a):
        return a.reshape(b, n, HG_CHUNK, h, a.shape[-1]).swapaxes(0, 1)

    causal = jnp.tril(jnp.ones((HG_CHUNK, HG_CHUNK), dtype=bool))[None, :, :, None, None]

    def step(state, inp):
        qc, kc, vc, gc = inp
        bcum = jnp.cumsum(gc, axis=1)
        total = bcum[:, -1]
        new_state = jnp.exp(total)[..., None] * state + jnp.einsum(
            'bshk,bshv->bhkv', kc * jnp.exp(total[:, None] - bcum), vc)
        if not with_output:
            return new_state, None
        decay = jnp.exp(jnp.where(causal, bcum[:, :, None] - bcum[:, None], -jnp.inf))
        scores = jnp.einsum('bthk,btshk,bshk->bhts', qc, decay, kc)
        o = jnp.einsum('bhts,bshv->bthv', scores, vc) + jnp.einsum(
            'bthk,bhkv->bthv', qc * jnp.exp(bcum), state)
        return new_state, o

    s_fin, o = lax.scan(step, s0, (to_chunks(q), to_chunks(k), to_chunks(v), to_chunks(log_f)))
    if with_output:
        o = o.swapaxes(0, 1).reshape(b, t, h, v.shape[-1])
    return o, s_fin


def hgrn2_inputs(p, lb_fb):
    b, t, _ = p.shape
    q = jax.nn.silu(p[..., O_HQ:O_HFF].astype(jnp.float32)).reshape(b, t, HG_HEADS, HG_K)
    i = p[..., O_HI:O_HG].astype(jnp.float32).reshape(b, t, HG_HEADS, HG_V)

    def gate(z, lb):
        f = lb + (1.0 - lb) * jax.nn.sigmoid(z.astype(jnp.float32))
        return (1.0 - f).reshape(b, t, HG_HEADS, HG_K), jnp.log(f).reshape(b, t, HG_HEADS, HG_K)

    fwd = gate(p[..., O_HFF:O_HFB], lb_fb[0])
    bwd = gate(p[..., O_HFB:O_HI], lb_fb[1])
    return q, i, fwd, bwd


def hgrn2_readout(o, z_gate, gain):
    b, t = o.shape[0], o.shape[1]
    on = o * lax.rsqrt(jnp.mean(o * o, axis=-1, keepdims=True) + EPS)
    on = on * gain.astype(jnp.float32).reshape(HG_HEADS, HG_V)
    return (on.reshape(b, t, HG_WIDTH) * jax.nn.silu(z_gate.astype(jnp.float32))).astype(z_gate.dtype)


def fourier_mix(z, w):
    b, t, _ = z.shape
    zg = z.astype(jnp.float32).reshape(b, t, FN_GROUPS, FN_GROUP_DIM)
    mixed = jnp.fft.fft2(zg, axes=(1, 3), norm='ortho').real
    return mixed.reshape(b, t, FN_WIDTH).astype(z.dtype) @ w


def swiglu_ffn(h, w_gu, w_dn):
    gate, up = jnp.split(h @ w_gu, 2, axis=-1)
    return (jax.nn.silu(gate) * up) @ w_dn


def setup_inputs(seed: int = 0) -> dict:
    key = jax.random.key(seed)
    ks = jax.random.split(key, 20)
    f32 = jnp.float32

    def nrm(k, shape, scale):
        return jax.random.normal(k, shape, f32) * scale

    def gain(k, shape):
        return 1.0 + 0.02 * jax.random.normal(k, shape, f32)

    return {
        'x': nrm(ks[0], (BATCH, SEQ, D_MODEL), 1.0),
        'c': nrm(ks[1], (BATCH, D_MODEL), 1.0),
        'ctx': nrm(ks[2], (BATCH, CTX_LEN, D_MODEL), 1.0),
        'c_ctx': nrm(ks[3], (D_MODEL,), 1.0),
        'w_mod': nrm(ks[4], (DEPTH, D_MODEL, 6 * D_MODEL), D_MODEL ** -0.5),
        'b_mod': nrm(ks[5], (DEPTH, 6 * D_MODEL), 0.02),
        'norm1_g': gain(ks[6], (DEPTH, D_MODEL)),
        'norm2_g': gain(ks[7], (DEPTH, D_MODEL)),
        'w_in': nrm(ks[8], (DEPTH, D_MODEL, IN_WIDTH), D_MODEL ** -0.5),
        'q_norm_g': gain(ks[9], (DEPTH, MLA_Q_LORA)),
        'w_uq': nrm(ks[10], (DEPTH, MLA_Q_LORA, MLA_HEADS * (MLA_NOPE + MLA_ROPE)), MLA_Q_LORA ** -0.5),
        'kv_norm_g': gain(ks[11], (DEPTH, MLA_KV_LORA)),
        'w_ukv': nrm(ks[12], (DEPTH, MLA_KV_LORA, MLA_HEADS * (MLA_NOPE + MLA_V)), MLA_KV_LORA ** -0.5),
        'lb_param': nrm(ks[13], (DEPTH, 2, HG_HEADS * HG_K), 1.0),
        'hg_norm_g': gain(ks[14], (DEPTH, HG_WIDTH)),
        'w_fourier': nrm(ks[15], (DEPTH, FN_WIDTH, FN_WIDTH), FN_WIDTH ** -0.5),
        'w_out': nrm(ks[16], (DEPTH, MIX_WIDTH, D_MODEL), MIX_WIDTH ** -0.5),
        'w_gate_up': nrm(ks[17], (DEPTH, D_MODEL, 2 * FFN_HIDDEN), D_MODEL ** -0.5),
        'w_down': nrm(ks[18], (DEPTH, FFN_HIDDEN, D_MODEL), FFN_HIDDEN ** -0.5),
        'final_norm_g': gain(ks[19], (D_MODEL,)),
    }


def reference(x, c, ctx, c_ctx, w_mod, b_mod, norm1_g, norm2_g, w_in, q_norm_g, w_uq, kv_norm_g, w_ukv,
              lb_param, hg_norm_g, w_fourier, w_out, w_gate_up, w_down, final_norm_g):
    b, t, _ = x.shape
    rope = axial_rope(t, x.dtype)
    probs = jax.nn.softmax(lb_param.astype(jnp.float32), axis=0)
    lower_bounds = jnp.cumsum(probs, axis=0) - probs[0]
    s_zero = jnp.zeros((b, HG_HEADS, HG_K, HG_V), jnp.float32)
    c_act = jax.nn.silu(c)
    cc_act = jax.nn.silu(c_ctx)

    def flip(a):
        return jnp.flip(a, axis=1)

    for l in range(DEPTH):
        last = l == DEPTH - 1
        sh1, sc1, g1, sh2, sc2, g2 = jnp.split((c_act @ w_mod[l] + b_mod[l])[:, None, :], 6, axis=-1)
        csh1, csc1, cg1, csh2, csc2, cg2 = jnp.split((cc_act @ w_mod[l] + b_mod[l])[None, None, :], 6, axis=-1)

        p = modulate(x, norm1_g[l], sh1, sc1) @ w_in[l]
        pc = modulate(ctx, norm1_g[l], csh1, csc1) @ w_in[l]

        k_lat, v_lat = mla_keys_values(p, kv_norm_g[l], w_ukv[l], rope)
        k_ctx, v_ctx = mla_keys_values(pc, kv_norm_g[l], w_ukv[l], None)
        q_lat = mla_queries(p, q_norm_g[l], w_uq[l], rope)
        attn = block_attention(q_lat, jnp.concatenate([k_lat, k_ctx], axis=1),
                               jnp.concatenate([v_lat, v_ctx], axis=1))

        lb_fb = lower_bounds[l]
        qc, ic, (kcf, lcf), (kcb, lcb) = hgrn2_inputs(pc, lb_fb)
        o_cf, s_cf = chunk_gated_scan(qc, kcf, ic, lcf, s_zero, not last)
        o_cb, s_cb = chunk_gated_scan(flip(qc), flip(kcb), flip(ic), flip(lcb), s_zero, not last)
        ql, il, (klf, llf), (klb, llb) = hgrn2_inputs(p, lb_fb)
        o_lf, _ = chunk_gated_scan(ql, klf, il, llf, s_cf, True)
        o_lb, _ = chunk_gated_scan(flip(ql), flip(klb), flip(il), flip(llb), s_cb, True)
        hg = hgrn2_readout(o_lf + flip(o_lb), p[..., O_HG:O_FN], hg_norm_g[l])

        fn = fourier_mix(p[..., O_FN:IN_WIDTH], w_fourier[l])

        y = jnp.concatenate([attn, hg, fn], axis=-1) @ w_out[l]
        x_new = x + g1 * y
        x_new = x_new + g2 * swiglu_ffn(modulate(x_new, norm2_g[l], sh2, sc2), w_gate_up[l], w_down[l])

        if not last:
            attn_c = block_attention(mla_queries(pc, q_norm_g[l], w_uq[l], None), k_ctx, v_ctx)
            hg_c = hgrn2_readout(o_cf + flip(o_cb), pc[..., O_HG:O_FN], hg_norm_g[l])
            fn_c = fourier_mix(pc[..., O_FN:IN_WIDTH], w_fourier[l])
            yc = jnp.concatenate([attn_c, hg_c, fn_c], axis=-1) @ w_out[l]
            ctx = ctx + cg1 * yc
            ctx = ctx + cg2 * swiglu_ffn(modulate(ctx, norm2_g[l], csh2, csc2), w_gate_up[l], w_down[l])
        x = x_new

    return rms_norm(x, final_norm_g)
```

```python
import contextlib
import math
import numpy as np
import ml_dtypes
import concourse.bass as bass
import concourse.mybir as mybir
from concourse.bass_utils import run_bass_kernel_spmd

F32 = mybir.dt.float32
BF16 = mybir.dt.bfloat16
ALU = mybir.AluOpType
AF = mybir.ActivationFunctionType
AX = mybir.AxisListType

D = 1024
NCORES = 8
BATCH = 32
SEQ = 2048
LC = 256
NTOK = SEQ + LC
NT = NTOK // 128
CH = 32
NCH = NTOK // CH
DEPTH = 2
EPS = 1e-6
FFH = 2816
NF = FFH // 128
IN_WIDTH = 2752
O_CQ, O_CKV, O_KR, O_HQ, O_HFF, O_HFB, O_HI, O_HG, O_FN = 0, 256, 384, 448, 960, 1472, 1984, 2240, 2496
TBLK = [(0, 256), (256, 512), (768, 512), (1280, 512), (1792, 512)]


class T:
    __slots__ = ("ap", "w", "r", "name", "excl")

    def __init__(self, ap, name="", excl=False):
        self.ap = ap
        self.w = None
        self.r = {}
        self.name = name
        self.excl = excl

    def __getitem__(self, k):
        return self.ap[k]


class Eng:
    def __init__(self, kb, name, eng):
        self.name = name
        self.eng = eng
        self.sem = kb.new_sem("e_" + name)
        self.cnt = 0
        self.waited = {}


class KB:
    def __init__(self):
        self.nc = bass.Bass("TRN2", target_bir_lowering=False)
        self.es = contextlib.ExitStack()
        self.nsem = 0
        self.dsem = {}
        nc = self.nc
        self.PE = Eng(self, "pe", nc.tensor)
        self.ACT = Eng(self, "act", nc.scalar)
        self.DVE = Eng(self, "dve", nc.vector)
        self.POOL = Eng(self, "pool", nc.gpsimd)
        self.SP = Eng(self, "sp", nc.sync)
        self.engs = [self.PE, self.ACT, self.DVE, self.POOL, self.SP]
        self.ninst = 0
        self.scopes = []

    def new_sem(self, name):
        self.nsem += 1
        return self.es.enter_context(self.nc.semaphore(name))

    def _stack(self):
        return self.scopes[-1] if self.scopes else self.es

    def sb(self, name, shape, dt=F32):
        self.uid = getattr(self, "uid", 0) + 1
        return self._stack().enter_context(self.nc.sbuf_tensor(f"{name}_{self.uid}", list(shape), dt))

    def sbT(self, name, shape, dt=F32):
        t = self.sb(name, shape, dt)
        return T(t, name)

    def ps(self, name, shape, dt=F32):
        return self._stack().enter_context(self.nc.psum_tensor(name, list(shape), dt))

    def dram(self, name, shape, dt, kind=None):
        if kind is None:
            return self.nc.dram_tensor(name, list(shape), dt).ap()
        return self.nc.dram_tensor(name, list(shape), dt, kind=kind).ap()

    @contextlib.contextmanager
    def scope(self):
        st = contextlib.ExitStack()
        self.scopes.append(st)
        try:
            yield
        finally:
            self.barrier()
            self.scopes.pop()
            st.close()

    def _needs(self, reads, writes):
        needs = {}

        def add(ev):
            if ev is None:
                return
            key = id(ev[0])
            if key not in needs or needs[key][1] < ev[1]:
                needs[key] = ev
        for t in reads:
            add(t.w)
            if t.excl:
                for ev in t.r.values():
                    add(ev)
        for t in writes:
            add(t.w)
            for ev in t.r.values():
                add(ev)
        return needs

    def _sync(self, E, reads, writes):
        for key, (sem, val, grp) in self._needs(reads, writes).items():
            if sem is E.sem and E is self.PE:
                continue
            if grp is not None:
                val = self.dsem[grp][1]
            if E.waited.get(key, 0) >= val:
                continue
            E.eng.wait_ge(sem, val)
            E.waited[key] = val
            self.ninst += 1

    def _record(self, ev, reads, writes):
        key = id(ev[0])
        for t in reads:
            t.r[key] = ev
        for t in writes:
            t.w = ev
            t.r = {}

    def op(self, E, fn, reads=(), writes=()):
        self._sync(E, reads, writes)
        inst = fn(E.eng)
        E.cnt += 1
        inst.then_inc(E.sem, 1)
        self.ninst += 1
        self._record((E.sem, E.cnt, None), reads, writes)
        return inst

    def mm(self, pst, out_ap, pairs, reads, start=True, stop=True):
        E = self.PE
        self._sync(E, reads, [pst])
        n = len(pairs)
        inst = None
        for i, (l, r) in enumerate(pairs):
            inst = self.nc.tensor.matmul(out_ap, lhsT=l, rhs=r, start=(start and i == 0), stop=(stop and i == n - 1))
            self.ninst += 1
        E.cnt += 1
        inst.then_inc(E.sem, 1)
        self._record((E.sem, E.cnt, None), reads, [pst])

    def mm_multi(self, pst, groups, reads):
        E = self.PE
        self._sync(E, reads, [pst])
        inst = None
        for out_ap, pairs in groups:
            n = len(pairs)
            for i, (l, r) in enumerate(pairs):
                inst = self.nc.tensor.matmul(out_ap, lhsT=l, rhs=r, start=(i == 0), stop=(i == n - 1))
                self.ninst += 1
        E.cnt += 1
        inst.then_inc(E.sem, 1)
        self._record((E.sem, E.cnt, None), reads, [pst])

    def tr(self, pst, outs_ins, ident, reads):
        E = self.PE
        self._sync(E, list(reads) + [ident], [pst])
        inst = None
        for o, i in outs_ins:
            inst = self.nc.tensor.transpose(o, i, ident.ap[0:i.shape[0], 0:i.shape[0]])
            self.ninst += 1
        E.cnt += 1
        inst.then_inc(E.sem, 1)
        self._record((E.sem, E.cnt, None), list(reads) + [ident], [pst])

    def dma(self, Q, grp, out, in_, reads=(), writes=(), slow=False):
        if grp not in self.dsem:
            self.dsem[grp] = [self.new_sem("d_" + grp), 0]
        self._sync(Q, reads, writes)
        if slow:
            inst = Q.eng.dma_start(out=out, in_=in_, allow_slow_non_contiguous=True)
        else:
            inst = Q.eng.dma_start(out=out, in_=in_)
        ds = self.dsem[grp]
        ds[1] += 16
        inst.then_inc(ds[0], 16)
        self.ninst += 1
        self._record((ds[0], ds[1], grp), reads, writes)

    def barrier(self):
        for E in self.engs:
            for O in self.engs:
                if O is E or O.cnt == 0:
                    continue
                key = id(O.sem)
                if E.waited.get(key, 0) < O.cnt:
                    E.eng.wait_ge(O.sem, O.cnt)
                    E.waited[key] = O.cnt
            for grp, (sem, cnt) in self.dsem.items():
                key = id(sem)
                if cnt and E.waited.get(key, 0) < cnt:
                    E.eng.wait_ge(sem, cnt)
                    E.waited[key] = cnt

    def finish(self):
        self.barrier()


class Rot:
    def __init__(self, items):
        self.items = items
        self.i = 0

    def next(self):
        t = self.items[self.i % len(self.items)]
        self.i += 1
        return t


def _bf(a):
    return np.ascontiguousarray(a.astype(np.float32)).astype(ml_dtypes.bfloat16)


def make_consts():
    c = {}
    c["ident"] = _bf(np.eye(128))
    c["ones"] = _bf(np.ones((128, 128)))
    rows = SEQ // 64
    row_pos = np.repeat(np.arange(rows, dtype=np.float32), 64)
    col_pos = np.tile(np.arange(64, dtype=np.float32), rows)
    inv_freq = (np.float32(10000.0) ** (-np.arange(0, 32, 2, dtype=np.float32) / np.float32(32))).astype(np.float32)
    ang_r = row_pos[:, None] * inv_freq
    ang_c = col_pos[:, None] * inv_freq
    ang = np.concatenate([ang_r, ang_r, ang_c, ang_c], axis=-1).astype(np.float32)
    cos = np.cos(ang).astype(np.float32).T
    sin = np.sin(ang).astype(np.float32).T.copy()
    sin[0:16] *= -1.0
    sin[32:48] *= -1.0
    c["cos"] = np.ascontiguousarray(cos)
    c["sin"] = np.ascontiguousarray(sin)
    def dft(n, scale):
        k = np.arange(n, dtype=np.int64)
        m = (k[:, None] * k[None, :]) % n
        a = 2.0 * np.pi * m.astype(np.float64) / n
        return np.cos(a) * scale, -np.sin(a) * scale
    ct, nst = dft(SEQ, 1.0 / math.sqrt(SEQ))
    c["dftc"] = _bf(ct)
    c["dfts"] = _bf(nst)
    cc, nsc = dft(LC, 1.0 / math.sqrt(LC))
    c["dftc_c"] = _bf(cc)
    c["dfts_c"] = _bf(nsc)
    c64, ns64 = dft(64, 1.0 / 8.0)
    bdc = np.zeros((256, 256))
    bds = np.zeros((256, 256))
    for g in range(4):
        bdc[g * 64:(g + 1) * 64, g * 64:(g + 1) * 64] = c64
        bds[g * 64:(g + 1) * 64, g * 64:(g + 1) * 64] = -ns64
    c["bd"] = _bf(np.concatenate([bdc, bds], axis=1))
    s = np.arange(CH)
    c["maskf"] = (s[:, None] <= s[None, :]).astype(np.float32)
    c["maskb"] = (s[:, None] >= s[None, :]).astype(np.float32)
    return c


CONST_SPECS = [("ident", [128, 128], BF16), ("ones", [128, 128], BF16), ("cos", [64, SEQ], F32), ("sin", [64, SEQ], F32),
               ("dftc", [SEQ, SEQ], BF16), ("dfts", [SEQ, SEQ], BF16), ("dftc_c", [LC, LC], BF16), ("dfts_c", [LC, LC], BF16),
               ("bd", [256, 512], BF16), ("maskf", [CH, CH], F32), ("maskb", [CH, CH], F32)]

INPUT_SPECS = [("x", [None, SEQ, D]), ("c", [None, D]), ("ctx", [None, LC, D]), ("c_ctx", [D]),
               ("w_mod", [DEPTH, D, 6 * D]), ("b_mod", [DEPTH, 6 * D]), ("norm1_g", [DEPTH, D]), ("norm2_g", [DEPTH, D]),
               ("w_in", [DEPTH, D, IN_WIDTH]), ("q_norm_g", [DEPTH, 256]), ("w_uq", [DEPTH, 256, 768]),
               ("kv_norm_g", [DEPTH, 128]), ("w_ukv", [DEPTH, 128, 1024]), ("lb_param", [DEPTH, 2, 512]),
               ("hg_norm_g", [DEPTH, 256]), ("w_fourier", [DEPTH, 256, 256]), ("w_out", [DEPTH, D, D]),
               ("w_gate_up", [DEPTH, D, 2 * FFH]), ("w_down", [DEPTH, FFH, D]), ("final_norm_g", [D])]


class Prog:
    def __init__(self, NB=4, layers=(0, 1), debug=None, stop_after=None):
        self.NB = NB
        self.layers = tuple(layers)
        self.debug = debug or {}
        self.stop_after = stop_after
        kb = self.kb = KB()
        self.I = {}
        for name, shape in INPUT_SPECS:
            shape = [NB if s is None else s for s in shape]
            self.I[name] = kb.dram(name, shape, F32, kind="ExternalInput")
        self.C = {name: kb.dram("k_" + name, shape, dt, kind="ExternalInput") for name, shape, dt in CONST_SPECS}
        self.out = kb.dram("out", [NB, SEQ, D], F32, kind="ExternalOutput")
        self.outT = T(self.out, "out")
        self.dbg = {}
        for name, (shape, dt) in self.debug.items():
            self.dbg[name] = kb.dram("dbg_" + name, shape, dt, kind="ExternalOutput")
        self.W = {}
        for l in range(DEPTH):
            w = {}
            for nm, shape in [("WA", [D, 512]), ("WH", [D, 2048]), ("WF", [D, 256]), ("WUQ", [256, 1024]),
                              ("WUKV", [128, 1024]), ("WFO", [256, 256]), ("WO", [D, D]), ("WGU", [D, 2 * FFH]), ("WDN", [FFH, D])]:
                w[nm] = T(kb.dram(f"s_{nm}{l}", shape, BF16), f"{nm}{l}")
            self.W[l] = w
        self.MOD = [T(kb.dram(f"s_mod{l}", [5, 6 * D], F32), f"mod{l}") for l in range(DEPTH)]
        self.xs = [T(kb.dram(f"s_xs{b}", [NTOK, D], F32), f"xs{b}") for b in range(NB)]
        self.mixT = [T(kb.dram(f"s_mix{c}", [128, NTOK], BF16), f"mix{c}") for c in range(8)]
        self.ident = kb.sbT("ident", [128, 128], BF16)
        self.ones = kb.sbT("ones", [128, 128], BF16)
        self.epsT = kb.sbT("epsT", [128, 1], F32)
        self.cact = kb.sbT("cact", [128, 5, 8], F32)
        self.lb = kb.sbT("lb", [128, 2, 8], F32)
        self.lb1 = kb.sbT("lb1", [128, 2, 8], F32)
        self.psb = [T(kb.ps(f"psb{i}", [128, 512], F32), f"psb{i}", excl=True) for i in range(8)]
        self.pr = Rot(self.psb)
        kb.dma(kb.SP, "const", self.ident[:], self.C["ident"][:, :], writes=[self.ident])
        kb.dma(kb.SP, "const", self.ones[:], self.C["ones"][:, :], writes=[self.ones])
        kb.op(kb.DVE, lambda e: e.memset(self.epsT[:], EPS), writes=[self.epsT])

    def psbf(self, pt):
        return pt.ap[:].bitcast(BF16)

    def rstd(self, out, ss, tmp, inv_n, Ts, rT=()):
        kb = self.kb
        rT = list(rT)
        kb.op(kb.DVE, lambda e: e.tensor_scalar(out=tmp, in0=ss, scalar1=inv_n, scalar2=EPS, op0=ALU.mult, op1=ALU.add), reads=rT + list(Ts), writes=Ts)
        kb.op(kb.ACT, lambda e: e.activation(out=tmp, in_=tmp, func=AF.Sqrt), reads=Ts, writes=Ts)
        kb.op(kb.DVE, lambda e: e.reciprocal(out=out, in_=tmp), reads=Ts, writes=Ts)

    def copy(self, E, out, in_, reads, writes):
        kb = self.kb
        if E is kb.ACT:
            kb.op(E, lambda e: e.activation(out=out, in_=in_, func=AF.Copy), reads=reads, writes=writes)
        else:
            kb.op(E, lambda e: e.tensor_copy(out=out, in_=in_), reads=reads, writes=writes)

    def prs(self, a, b):
        return Rot(self.psb[a:b])

    def wload(self, dstT, srcT, kc):
        kb = self.kb
        if kc == 1:
            kb.dma(kb.SP, "w", dstT[:], srcT.ap[:, :], reads=[srcT], writes=[dstT])
        else:
            kb.dma(kb.SP, "w", dstT[:], srcT.ap.rearrange("(kc p) n -> p kc n", p=128), reads=[srcT], writes=[dstT])

    def tap(self, name, src_ap, srcT):
        if name in self.dbg:
            self.kb.dma(self.kb.SP, "dbg", self.dbg[name], src_ap, reads=[srcT])

    def convert_weights(self):
        kb = self.kb
        I = self.I
        with kb.scope():
            stg = Rot([kb.sbT(f"stg{i}", [128, 11264], BF16) for i in range(3)])

            def conv(dstT, src2d, K, pieces, cap):
                KC = K // 128
                srcv = src2d.rearrange("(kc p) n -> p kc n", p=128)
                dstv = dstT.ap.rearrange("(kc p) n -> p kc n", p=128)
                groups, cur, tot = [], [], 0
                for (dc, sc, n) in pieces:
                    if tot + n > cap:
                        groups.append(cur)
                        cur, tot = [], 0
                    cur.append((dc, sc, n))
                    tot += n
                if cur:
                    groups.append(cur)
                for g in groups:
                    st = stg.next()
                    tot = sum(n for _, _, n in g)
                    sv = st.ap[:, 0:KC * tot].rearrange("p (kc n) -> p kc n", kc=KC)
                    off = 0
                    for (dc, sc, n) in g:
                        kb.dma(kb.POOL, "cv_in", sv[:, :, off:off + n], srcv[:, :, sc:sc + n], writes=[st])
                        off += n
                    off = 0
                    for (dc, sc, n) in g:
                        kb.dma(kb.SP, "cv_out", dstv[:, :, dc:dc + n], sv[:, :, off:off + n], reads=[st], writes=[dstT])
                        off += n

            def split(total, step, d0=0, s0=0):
                return [(d0 + o, s0 + o, min(step, total - o)) for o in range(0, total, step)]

            rot = [(0, 16, 16), (16, 0, 16), (32, 48, 16), (48, 32, 16)]
            for l in self.layers:
                W = self.W[l]
                win = I["w_in"][l]
                conv(W["WA"], win, D, [(0, 0, 448)] + [(448 + d, O_KR + s, n) for d, s, n in rot], 1024)
                pcs = []
                for h in range(4):
                    pcs += [(h * 512, O_HQ + h * 128, 128), (h * 512 + 128, O_HFF + h * 128, 128), (h * 512 + 256, O_HFB + h * 128, 128),
                            (h * 512 + 384, O_HG + h * 64, 64), (h * 512 + 448, O_HI + h * 64, 64)]
                conv(W["WH"], win, D, pcs, 1024)
                conv(W["WF"], win, D, [(0, O_FN, 256)], 1024)
                pcs = []
                for h in range(4):
                    pcs.append((h * 256, h * 192, 192))
                    pcs += [(h * 256 + 192 + d, h * 192 + 128 + s, n) for d, s, n in rot]
                conv(W["WUQ"], I["w_uq"][l], 256, pcs, 2048)
                pcs = []
                for h in range(4):
                    pcs += [(h * 128, h * 256, 128), (512 + h * 128, h * 256 + 128, 128)]
                conv(W["WUKV"], I["w_ukv"][l], 128, pcs, 2048)
                conv(W["WFO"], I["w_fourier"][l], 256, [(0, 0, 256)], 2048)
                conv(W["WO"], I["w_out"][l], D, split(D, 1024), 1024)
                conv(W["WGU"], I["w_gate_up"][l], D, split(2 * FFH, 1024), 1024)
                conv(W["WDN"], I["w_down"][l], FFH, split(D, 512), 512)

    def phase_cact(self):
        kb = self.kb
        I = self.I
        for m in range(self.NB):
            kb.dma(kb.SP, "cact", self.cact[:, m, :], I["c"][m].rearrange("(p j) -> p j", j=8), writes=[self.cact])
        for m in range(self.NB, 5):
            kb.dma(kb.SP, "cact", self.cact[:, m, :], I["c_ctx"].rearrange("(p j) -> p j", j=8), writes=[self.cact])
        kb.op(kb.ACT, lambda e: e.activation(out=self.cact[:], in_=self.cact[:], func=AF.Silu), reads=[self.cact], writes=[self.cact])
        with kb.scope():
            p0 = kb.sbT("lbp0", [128, 8], F32)
            p1 = kb.sbT("lbp1", [128, 8], F32)
            for d in range(2):
                kb.dma(kb.SP, "cact", p0[:, d * 4:(d + 1) * 4], I["lb_param"][0, d].rearrange("(h p) -> p h", p=128), writes=[p0], slow=True)
                kb.dma(kb.SP, "cact", p1[:, d * 4:(d + 1) * 4], I["lb_param"][1, d].rearrange("(h p) -> p h", p=128), writes=[p1], slow=True)
            kb.op(kb.DVE, lambda e: e.tensor_tensor(out=p1[:], in0=p1[:], in1=p0[:], op=ALU.subtract), reads=[p0, p1], writes=[p1])
            kb.op(kb.ACT, lambda e: e.activation(out=self.lb1[:, 0, :], in_=p1[:], func=AF.Sigmoid), reads=[p1], writes=[self.lb1])
            kb.op(kb.DVE, lambda e: e.tensor_scalar(out=self.lb1[:, 1, :], in0=self.lb1[:, 0, :], scalar1=-1.0, scalar2=1.0, op0=ALU.mult, op1=ALU.add),
                  reads=[self.lb1], writes=[self.lb1])
            kb.op(kb.DVE, lambda e: e.memset(self.lb[:, 0, :], 0.0), writes=[self.lb])
            kb.op(kb.DVE, lambda e: e.memset(self.lb[:, 1, :], 1.0), writes=[self.lb])

    def phase_mod(self, l):
        kb = self.kb
        I = self.I
        with kb.scope():
            wb = Rot([kb.sbT(f"wm{i}", [128, 8, 512], F32) for i in range(2)])
            mod = kb.sbT("mod_sb", [5, 6 * D], F32)
            bm = kb.sbT("bm", [5, 6 * D], F32)
            g1 = kb.sbT("n1g5", [5, D], F32)
            g2 = kb.sbT("n2g5", [5, D], F32)
            kb.dma(kb.SP, "modm", bm[:], I["b_mod"][l:l + 1, :].partition_broadcast(5), writes=[bm])
            kb.dma(kb.SP, "modm", g1[:], I["norm1_g"][l:l + 1, :].partition_broadcast(5), writes=[g1])
            kb.dma(kb.SP, "modm", g2[:], I["norm2_g"][l:l + 1, :].partition_broadcast(5), writes=[g2])
            wv = I["w_mod"][l].rearrange("(p j) n -> p j n", j=8)
            for nb in range(12):
                w = wb.next()
                kb.dma(kb.SP, "modw", w[:], wv[:, :, nb * 512:(nb + 1) * 512], writes=[w])
                pt = self.pr.next()
                kb.mm(pt, pt[0:5, :], [(self.cact[:, :, j], w[:, j, :]) for j in range(8)], reads=[self.cact, w])
                kb.op(kb.DVE, lambda e: e.tensor_tensor(out=mod[:, nb * 512:(nb + 1) * 512], in0=pt[0:5, :], in1=bm[:, nb * 512:(nb + 1) * 512], op=ALU.add),
                      reads=[pt, bm], writes=[mod])
            for slot, g in ((1, g1), (4, g2)):
                sl = mod[:, slot * D:(slot + 1) * D]
                kb.op(kb.DVE, lambda e: e.scalar_tensor_tensor(out=sl, in0=sl, scalar=1.0, in1=g[:], op0=ALU.add, op1=ALU.mult),
                      reads=[mod, g], writes=[mod])
            kb.dma(kb.SP, "modo", self.MOD[l].ap[:, :], mod[:], reads=[mod], writes=[self.MOD[l]])
            self.tap(f"mod{l}", mod[:], mod)

    def load_bc(self, dstT, l, m, slot):
        self.kb.dma(self.kb.SP, "bc", dstT[:], self.MOD[l].ap[m:m + 1, slot * D:(slot + 1) * D].partition_broadcast(128),
                    reads=[self.MOD[l]], writes=[dstT])

    def norm_res(self, pfx):
        kb = self.kb
        r = {}
        r["junk"] = kb.sbT(pfx + "junk", [128, D], F32)
        r["st"] = Rot([kb.sbT(f"{pfx}st{i}", [128, 4], F32) for i in range(3)])
        r["tmp"] = Rot([kb.sbT(f"{pfx}tmp{i}", [128, D], F32) for i in range(2)])
        r["xm"] = Rot([kb.sbT(f"{pfx}xm{i}", [128, D], BF16) for i in range(2)])
        return r

    def norm_tile(self, r, xt, S, H, dstT, dst_ap):
        kb = self.kb
        st = r["st"].next()
        tmp = r["tmp"].next()
        xm = r["xm"].next()
        junk = r["junk"]
        kb.op(kb.ACT, lambda e: e.activation(out=junk[:], in_=xt[:], func=AF.Square, accum_out=st[:, 0:1]), reads=[xt], writes=[junk, st])
        self.rstd(st[:, 2:3], st[:, 0:1], st[:, 1:2], 1.0 / D, [st])
        kb.op(kb.DVE, lambda e: e.scalar_tensor_tensor(out=tmp[:], in0=xt[:], scalar=st[:, 2:3], in1=S[:], op0=ALU.mult, op1=ALU.mult),
              reads=[xt, st, S], writes=[tmp])
        kb.op(kb.DVE, lambda e: e.tensor_tensor(out=xm[:], in0=tmp[:], in1=H[:], op=ALU.add), reads=[tmp, H], writes=[xm])
        pt = self.pr.next()
        pv = self.psbf(pt)
        kb.tr(pt, [(pv[:, c * 128:(c + 1) * 128], xm[:, c * 128:(c + 1) * 128]) for c in range(8)], self.ident, reads=[xm])
        kb.op(kb.ACT, lambda e: e.activation(out=dst_ap, in_=pv.rearrange("p (c t) -> p c t", c=8), func=AF.Copy), reads=[pt], writes=[dstT])

    def x_src(self, l, b, i):
        if l == self.layers[0] and l == 0:
            if i < 2:
                return self.I["ctx"][b, i * 128:(i + 1) * 128, :], None
            return self.I["x"][b, (i - 2) * 128:(i - 1) * 128, :], None
        return self.xs[b].ap[i * 128:(i + 1) * 128, :], self.xs[b]

    def phase_A(self, l, b, xmodT, xmT):
        kb = self.kb
        with kb.scope():
            r = self.norm_res("a_")
            xin = Rot([kb.sbT(f"a_xin{i}", [128, D], F32) for i in range(3)])
            SH = [kb.sbT(f"a_sh{i}", [128, D], F32) for i in range(4)]
            self.load_bc(SH[0], l, b, 1)
            self.load_bc(SH[1], l, b, 0)
            self.load_bc(SH[2], l, 4, 1)
            self.load_bc(SH[3], l, 4, 0)
            for i in range(NT):
                xt = xin.next()
                src, srcT = self.x_src(l, b, i)
                kb.dma(kb.SP, "xin", xt[:], src, reads=[srcT] if srcT else [], writes=[xt])
                S, H = (SH[2], SH[3]) if i < 2 else (SH[0], SH[1])
                self.norm_tile(r, xt, S, H, xmT[i], xmodT[:, :, i * 128:(i + 1) * 128])

    def phase_MLA(self, l, b, xmodT, xmT, last):
        kb = self.kb
        W = self.W[l]
        I = self.I
        SC = 1.0 / math.sqrt(192.0)

        def blk_tiles(bi):
            return [0, 1] if bi == 0 else list(range(2 + 4 * (bi - 1), 6 + 4 * (bi - 1)))

        def tile_blk(j):
            return 0 if j < 2 else 1 + (j - 2) // 4
        with kb.scope():
            WA = kb.sbT("m_WA", [128, 8, 512], BF16)
            WUQ = kb.sbT("m_WUQ", [128, 2, 1024], BF16)
            WUKV = kb.sbT("m_WUKV", [128, 1024], BF16)
            self.wload(WA, W["WA"], 8)
            self.wload(WUQ, W["WUQ"], 2)
            self.wload(WUKV, W["WUKV"], 1)
            gq = kb.sbT("m_gq", [128, 2], F32)
            gkv = kb.sbT("m_gkv", [128, 1], F32)
            kb.dma(kb.SP, "w", gq[:], I["q_norm_g"][l].rearrange("(c p) -> p c", p=128), writes=[gq], slow=True)
            kb.dma(kb.SP, "w", gkv[:], I["kv_norm_g"][l].rearrange("(c p) -> p c", p=128), writes=[gkv], slow=True)
            cos = kb.sbT("m_cos", [64, SEQ], F32)
            sin = kb.sbT("m_sin", [64, SEQ], F32)
            kb.dma(kb.SP, "w", cos[:], self.C["cos"][:, :], writes=[cos])
            kb.dma(kb.SP, "w", sin[:], self.C["sin"][:, :], writes=[sin])
            cqnT = kb.sb("m_cqnT", [128, 2, NTOK], BF16)
            ckvnT = kb.sb("m_ckvnT", [128, NTOK], BF16)
            kpeT = kb.sb("m_kpeT", [64, NTOK], BF16)
            knT = kb.sb("m_knT", [128, 4, NTOK], BF16)
            Vall = kb.sb("m_V", [128, NT, 512], BF16)
            cqT = [T(cqnT, f"cq{i}") for i in range(5)]
            ckvT = [T(ckvnT, f"ckv{i}") for i in range(5)]
            kpT = [T(kpeT, f"kp{i}") for i in range(5)]
            knTT = [T(knT, f"kn{i}") for i in range(5)]
            VT = [T(Vall, f"V{i}") for i in range(NT)]
            sq = Rot([kb.sbT(f"m_sq{i}", [128, 512], BF16) for i in range(3)])
            rq = kb.sbT("m_rq", [128, 512], F32)
            rqt = kb.sbT("m_rqt", [128, 512], F32)
            rkv = kb.sbT("m_rkv", [128, 512], F32)
            rkvt = kb.sbT("m_rkvt", [128, 512], F32)
            ra = Rot([kb.sbT(f"m_ra{i}", [64, 512], F32) for i in range(2)])
            rb = Rot([kb.sbT(f"m_rb{i}", [64, 512], F32) for i in range(2)])
            pr = self.prs(0, 8)

            def rope(dst_ap, dstT, pa, pb, pos0, wd):
                a = ra.next()
                bb = rb.next()
                kb.op(kb.DVE, lambda e: e.tensor_tensor(out=a[:, 0:wd], in0=pa[0:64, 0:wd], in1=cos[:, pos0:pos0 + wd], op=ALU.mult), reads=[pa, cos], writes=[a])
                kb.op(kb.DVE, lambda e: e.tensor_tensor(out=bb[:, 0:wd], in0=pb[0:64, 0:wd], in1=sin[:, pos0:pos0 + wd], op=ALU.mult), reads=[pb, sin], writes=[bb])
                kb.op(kb.DVE, lambda e: e.tensor_tensor(out=dst_ap, in0=a[:, 0:wd], in1=bb[:, 0:wd], op=ALU.add), reads=[a, bb], writes=[dstT])

            for bi, (s0, wd) in enumerate(TBLK):
                xr = [xmT[i] for i in blk_tiles(bi)]
                pts = [pr.next() for _ in range(5)]
                for gi, (c0, m) in enumerate([(0, 128), (128, 128), (256, 128), (384, 64), (448, 64)]):
                    kb.mm(pts[gi], pts[gi][0:m, 0:wd], [(WA[:, c, c0:c0 + m], xmodT[:, c, s0:s0 + wd]) for c in range(8)], reads=[WA] + xr)
                sqs = []
                for gi in range(3):
                    q = sq.next()
                    kb.op(kb.ACT, lambda e: e.activation(out=q[:, 0:wd], in_=pts[gi][:, 0:wd], func=AF.Square), reads=[pts[gi]], writes=[q])
                    sqs.append(q)
                ssq = pr.next()
                sskv = pr.next()
                kb.mm(ssq, ssq[:, 0:wd], [(self.ones[:], sqs[0][:, 0:wd]), (self.ones[:], sqs[1][:, 0:wd])], reads=[self.ones, sqs[0], sqs[1]])
                kb.mm(sskv, sskv[:, 0:wd], [(self.ones[:], sqs[2][:, 0:wd])], reads=[self.ones, sqs[2]])
                self.rstd(rq[:, 0:wd], ssq[:, 0:wd], rqt[:, 0:wd], 1.0 / 256, [rq, rqt], [ssq])
                self.rstd(rkv[:, 0:wd], sskv[:, 0:wd], rkvt[:, 0:wd], 1.0 / 128, [rkv, rkvt], [sskv])
                for c in range(2):
                    kb.op(kb.DVE, lambda e: e.scalar_tensor_tensor(out=cqnT[:, c, s0:s0 + wd], in0=pts[c][:, 0:wd], scalar=gq[:, c:c + 1], in1=rq[:, 0:wd],
                                                                   op0=ALU.mult, op1=ALU.mult), reads=[pts[c], gq, rq], writes=[cqT[bi]])
                kb.op(kb.DVE, lambda e: e.scalar_tensor_tensor(out=ckvnT[:, s0:s0 + wd], in0=pts[2][:, 0:wd], scalar=gkv[:, 0:1], in1=rkv[:, 0:wd],
                                                               op0=ALU.mult, op1=ALU.mult), reads=[pts[2], gkv, rkv], writes=[ckvT[bi]])
                if bi == 0:
                    self.copy(kb.ACT, kpeT[:, s0:s0 + wd], pts[3][0:64, 0:wd], [pts[3]], [kpT[bi]])
                else:
                    rope(kpeT[:, s0:s0 + wd], kpT[bi], pts[3], pts[4], s0 - LC, wd)
            for bi, (s0, wd) in enumerate(TBLK):
                for h in range(4):
                    pt = pr.next()
                    kb.mm(pt, pt[:, 0:wd], [(WUKV[:, h * 128:(h + 1) * 128], ckvnT[:, s0:s0 + wd])], reads=[WUKV, ckvT[bi]])
                    self.copy(kb.ACT, knT[:, h, s0:s0 + wd], pt[:, 0:wd], [pt], [knTT[bi]])
            for i in range(NT):
                pt = pr.next()
                kb.mm(pt, pt[:, :], [(ckvnT[:, i * 128:(i + 1) * 128], WUKV[:, 512:1024])], reads=[WUKV, ckvT[tile_blk(i)]])
                self.copy(kb.DVE, Vall[:, i, :], pt[:, :], [pt], [VT[i]])
            qn_b = [kb.sb(f"m_qn{i}", [128, NTOK], BF16) for i in range(2)]
            qp_b = [kb.sb(f"m_qp{i}", [64, NTOK], BF16) for i in range(2)]
            qnTs = [[T(qn_b[i], f"qn{i}_{j}") for j in range(5)] for i in range(2)]
            qpTs = [[T(qp_b[i], f"qp{i}_{j}") for j in range(5)] for i in range(2)]
            pTr = Rot([kb.sbT(f"m_pT{i}", [128, 512], BF16) for i in range(3)])
            rden = Rot([kb.sbT(f"m_rden{i}", [128, 512], F32) for i in range(2)])
            oT = Rot([kb.sbT(f"m_oT{i}", [128, 512], BF16) for i in range(2)])
            Sr = self.prs(0, 4)
            Or = self.prs(4, 6)
            Dr = self.prs(6, 8)
            qblocks = [1, 2, 3, 4] if last else [0, 1, 2, 3, 4]
            for h in range(4):
                qn, qp = qn_b[h % 2], qp_b[h % 2]
                qnT, qpT = qnTs[h % 2], qpTs[h % 2]
                for bi in qblocks:
                    s0, wd = TBLK[bi]
                    pt = Sr.next()
                    kb.mm(pt, pt[:, 0:wd], [(WUQ[:, c, h * 256:h * 256 + 128], cqnT[:, c, s0:s0 + wd]) for c in range(2)], reads=[WUQ, cqT[bi]])
                    self.copy(kb.ACT, qn[:, s0:s0 + wd], pt[:, 0:wd], [pt], [qnT[bi]])
                    pa = Sr.next()
                    kb.mm(pa, pa[0:64, 0:wd], [(WUQ[:, c, h * 256 + 128:h * 256 + 192], cqnT[:, c, s0:s0 + wd]) for c in range(2)], reads=[WUQ, cqT[bi]])
                    if bi == 0:
                        self.copy(kb.ACT, qp[:, s0:s0 + wd], pa[0:64, 0:wd], [pa], [qpT[bi]])
                    else:
                        pb = Sr.next()
                        kb.mm(pb, pb[0:64, 0:wd], [(WUQ[:, c, h * 256 + 192:h * 256 + 256], cqnT[:, c, s0:s0 + wd]) for c in range(2)], reads=[WUQ, cqT[bi]])
                        rope(qp[:, s0:s0 + wd], qpT[bi], pa, pb, s0 - LC, wd)
                for bi in qblocks:
                    s0, wd = TBLK[bi]
                    keys = [0, 1] if bi == 0 else list(range(NT))
                    po = Or.next()
                    pd = Dr.next()
                    for ji, j in enumerate(keys):
                        sT = Sr.next()
                        kb.mm(sT, sT[:, 0:wd], [(knT[:, h, j * 128:(j + 1) * 128], qn[:, s0:s0 + wd]), (kpeT[:, j * 128:(j + 1) * 128], qp[:, s0:s0 + wd])],
                              reads=[knTT[tile_blk(j)], kpT[tile_blk(j)], qnT[bi], qpT[bi]])
                        pT = pTr.next()
                        kb.op(kb.ACT, lambda e: e.activation(out=pT[:, 0:wd], in_=sT[:, 0:wd], func=AF.Exp, scale=SC), reads=[sT], writes=[pT])
                        kb.mm(po, po[:, 0:wd], [(Vall[:, j, h * 128:(h + 1) * 128], pT[:, 0:wd])], reads=[VT[j], pT], start=(ji == 0), stop=(ji == len(keys) - 1))
                        kb.mm(pd, pd[:, 0:wd], [(self.ones[:], pT[:, 0:wd])], reads=[self.ones, pT], start=(ji == 0), stop=(ji == len(keys) - 1))
                    rd = rden.next()
                    o = oT.next()
                    kb.op(kb.DVE, lambda e: e.reciprocal(out=rd[:, 0:wd], in_=pd[:, 0:wd]), reads=[pd], writes=[rd])
                    kb.op(kb.DVE, lambda e: e.tensor_tensor(out=o[:, 0:wd], in0=po[:, 0:wd], in1=rd[:, 0:wd], op=ALU.mult), reads=[po, rd], writes=[o])
                    kb.dma(kb.SP, "mixo", self.mixT[h].ap[:, s0:s0 + wd], o[:, 0:wd], reads=[o], writes=[self.mixT[h]])

    def phase_HG(self, l, b, xmodT, xmT, last):
        kb = self.kb
        W = self.W[l]
        I = self.I
        lbT = self.lb if l == 0 else self.lb1
        NCC = LC // CH
        SG8 = [(i, i + 8) for i in range(0, NCH, 8)]
        SG16 = [(0, 8)] + [(i, i + 16) for i in range(8, NCH, 16)]
        HV = 32

        def chunk_of(d, i):
            if d == 0:
                return i
            return NCC - 1 - i if i < NCC else NCH + NCC - 1 - i
        with kb.scope():
            WHr = Rot([kb.sbT(f"h_WH{i}", [128, 8, 512], BF16) for i in range(1)])
            whv = W["WH"].ap.rearrange("(kc p) n -> p kc n", p=128)
            mask = [kb.sbT("h_mf", [CH, CH], F32), kb.sbT("h_mb", [CH, CH], F32)]
            kb.dma(kb.SP, "w", mask[0][:], self.C["maskf"][:, :], writes=[mask[0]])
            kb.dma(kb.SP, "w", mask[1][:], self.C["maskb"][:, :], writes=[mask[1]])
            gcol = kb.sbT("h_gcol", [64, 4], F32)
            kb.dma(kb.SP, "w", gcol[:], I["hg_norm_g"][l].rearrange("(h v) -> v h", v=64), writes=[gcol], slow=True)
            one1 = kb.sbT("h_one", [128, 1], F32)
            kb.op(kb.DVE, lambda e: e.memset(one1[:], 1.0), writes=[one1])
            pr = self.prs(0, 8)
            Vc = kb.sb("h_Vc", [CH, NCH, 64], BF16)
            VcT = [T(Vc, f"Vc{g}") for g in range(NCH // 8)]
            zgT = kb.sbT("h_zgT", [64, NTOK], BF16)
            qS_t = kb.sb("h_qS", [128, NTOK], F32)
            bA_t = [kb.sb("h_bA0", [128, NTOK], F32), kb.sb("h_bA1", [128, NTOK], F32)]
            bB_t = kb.sb("h_bB", [128, NTOK], F32)
            bC_t = kb.sb("h_bC", [128, NTOK], F32)
            qS_T = [T(qS_t, f"qS{i}") for i in range(5)]
            bA_T = [[T(bA_t[d], f"bA{d}_{i}") for i in range(5)] for d in range(2)]
            bB_T = [T(bB_t, "bB")]
            bC_T = [T(bC_t, "bC")]
            qd = [kb.sbT(f"h_qd{d}", [128, NTOK], BF16) for d in range(2)]
            kd = [kb.sbT(f"h_kd{d}", [128, NTOK], BF16) for d in range(2)]
            ktm = kb.sbT("h_ktm", [CH, NCH, 128], BF16)
            oT = [kb.sbT(f"h_oT{d}", [64, NTOK], F32) for d in range(2)]
            ridx = kb.sbT("h_ridx", [128, NCH], F32)
            rslot = kb.sbT("h_rslot", [128, NCH], F32)
            dd = kb.sbT("h_dd", [128, NCH], F32)
            dmul = [kb.sbT(f"h_dmul{d}", [128, NCH], F32) for d in range(2)]
            d0t = [kb.sbT(f"h_d0{d}", [128, NCH], F32) for d in range(2)]
            Sbf = [kb.sbT(f"h_Sbf{d}", [128, NCH, 64], BF16) for d in range(2)]
            ATr = Rot([kb.sbT(f"h_AT{i}", [CH, 16, CH], BF16) for i in range(3)])
            rsr = Rot([kb.sbT(f"h_rs{i}", [64, 2, 512], F32) for i in range(1)])
            hgo = kb.sbT("h_hgo", [64, NTOK], BF16)
            pat = [kb.sbT("h_patlo", [128, CH], BF16), kb.sbT("h_pathi", [128, CH], BF16)]
            kb.op(kb.DVE, lambda e: e.memset(pat[0][:, 0:CH // 2], 1.0), writes=[pat[0]])
            kb.op(kb.DVE, lambda e: e.memset(pat[0][:, CH // 2:CH], 0.0), writes=[pat[0]])
            kb.op(kb.DVE, lambda e: e.memset(pat[1][:, 0:CH // 2], 0.0), writes=[pat[1]])
            kb.op(kb.DVE, lambda e: e.memset(pat[1][:, CH // 2:CH], 1.0), writes=[pat[1]])
            kA = kb.sbT("h_kA", [128, NTOK], BF16)
            kB = kb.sbT("h_kB", [128, NTOK], BF16)
            qB = kb.sbT("h_qB", [128, NTOK], BF16)
            for d in range(2):
                kb.op(kb.DVE, lambda e: e.memset(Sbf[d][:, 0, :], 0.0), writes=[Sbf[d]])
                kb.op(kb.DVE, lambda e: e.memset(dmul[d][:, NCH - 1:NCH], 1.0), writes=[dmul[d]])

            def v3(ap2d):
                return ap2d.rearrange("p (c t) -> p c t", t=CH)

            for h in range(4):
                WH = WHr.next()
                kb.dma(kb.SP, "w", WH[:], whv[:, :, h * 512:(h + 1) * 512], reads=[W["WH"]], writes=[WH])
                for g in range(NCH // 8):
                    pt = pr.next()
                    kb.mm_multi(pt, [(pt[0:CH, j * 64:(j + 1) * 64], [(xmodT[:, k, (g * 8 + j) * CH:(g * 8 + j + 1) * CH], WH[:, k, 448:512]) for k in range(8)])
                                     for j in range(8)], [WH, xmT[2 * g], xmT[2 * g + 1]])
                    self.copy(kb.DVE, Vc[:, g * 8:(g + 1) * 8, :], pt[0:CH, :].rearrange("p (j v) -> p j v", v=64), [pt], [VcT[g]])
                if getattr(self, 'hg_stop', 99) <= 1:
                    return
                for bi, (s0, wd) in enumerate(TBLK):
                    tl = [0, 1] if bi == 0 else list(range(2 + 4 * (bi - 1), 6 + 4 * (bi - 1)))
                    xr = [xmT[i] for i in tl]
                    pq, pf, pb, pg = pr.next(), pr.next(), pr.next(), pr.next()
                    for pt, c0, m in ((pq, 0, 128), (pf, 128, 128), (pb, 256, 128), (pg, 384, 64)):
                        kb.mm(pt, pt[0:m, 0:wd], [(WH[:, k, c0:c0 + m], xmodT[:, k, s0:s0 + wd]) for k in range(8)], reads=[WH] + xr)
                    kb.op(kb.ACT, lambda e: e.activation(out=qS_t[:, s0:s0 + wd], in_=pq[:, 0:wd], func=AF.Silu), reads=[pq], writes=[qS_T[bi]])
                    kb.op(kb.ACT, lambda e: e.activation(out=zgT[:, s0:s0 + wd], in_=pg[0:64, 0:wd], func=AF.Silu), reads=[pg], writes=[zgT])
                    kb.op(kb.ACT, lambda e: e.activation(out=bA_t[0][:, s0:s0 + wd], in_=pf[:, 0:wd], func=AF.Sigmoid), reads=[pf], writes=[bA_T[0][bi]])
                    kb.op(kb.ACT, lambda e: e.activation(out=bA_t[1][:, s0:s0 + wd], in_=pb[:, 0:wd], func=AF.Sigmoid), reads=[pb], writes=[bA_T[1][bi]])
                if getattr(self, 'hg_stop', 99) <= 2:
                    return
                for d in range(2):
                    bA = bA_t[d]
                    idx = d * 4 + h
                    lbc = lbT[:, 0, idx:idx + 1]
                    omc = lbT[:, 1, idx:idx + 1]
                    kb.op(kb.DVE, lambda e: e.tensor_scalar(out=bA[:], in0=bA[:], scalar1=omc, scalar2=lbc, op0=ALU.mult, op1=ALU.add),
                          reads=bA_T[d] + [lbT], writes=bA_T[d])
                    kb.op(kb.ACT, lambda e: e.activation(out=bB_t[:], in_=bA[:], func=AF.Ln), reads=bA_T[d], writes=bB_T)
                    kb.op(kb.DVE, lambda e: e.tensor_scalar(out=bA[:], in0=bA[:], scalar1=-1.0, scalar2=1.0, op0=ALU.mult, op1=ALU.add),
                          reads=bA_T[d], writes=bA_T[d])
                    ob = one1[:, 0:1]
                    if d == 0:
                        kb.op(kb.DVE, lambda e: e.tensor_tensor_scan(out=bC_t[:], data0=ob.to_broadcast([128, NTOK]), data1=bB_t[:], initial=0.0,
                                                                     op0=ALU.mult, op1=ALU.add), reads=bB_T + [one1], writes=bC_T)
                        mid = CH // 2 - 1
                    else:
                        kb.op(kb.DVE, lambda e: e.tensor_tensor_scan(out=bC_t[:, 0:LC][:, ::-1], data0=ob.to_broadcast([128, LC]), data1=bB_t[:, 0:LC][:, ::-1],
                                                                     initial=0.0, op0=ALU.mult, op1=ALU.add), reads=bB_T + [one1], writes=bC_T)
                        kb.op(kb.DVE, lambda e: e.tensor_tensor_scan(out=bC_t[:, LC:NTOK][:, ::-1], data0=ob.to_broadcast([128, SEQ]), data1=bB_t[:, LC:NTOK][:, ::-1],
                                                                     initial=bC_t[:, 0:1], op0=ALU.mult, op1=ALU.add), reads=bB_T + bC_T + [one1], writes=bC_T)
                        mid = CH // 2
                    c3 = v3(bC_t[:])
                    kb.op(kb.DVE, lambda e: e.tensor_copy(out=ridx[:], in_=c3[:, :, mid]), reads=bC_T, writes=[ridx])
                    if d == 0:
                        kb.op(kb.DVE, lambda e: e.tensor_copy(out=rslot[:], in_=ridx[:]), reads=[ridx], writes=[rslot])
                    else:
                        kb.op(kb.DVE, lambda e: e.tensor_copy(out=rslot[:, 0:NCC], in_=ridx[:, 0:NCC][:, ::-1]), reads=[ridx], writes=[rslot])
                        kb.op(kb.DVE, lambda e: e.tensor_copy(out=rslot[:, NCC:NCH], in_=ridx[:, NCC:NCH][:, ::-1]), reads=[ridx], writes=[rslot])
                    kb.op(kb.DVE, lambda e: e.tensor_tensor(out=dd[:, 0:NCH - 1], in0=rslot[:, 1:NCH], in1=rslot[:, 0:NCH - 1], op=ALU.subtract),
                          reads=[rslot], writes=[dd])
                    kb.op(kb.ACT, lambda e: e.activation(out=dmul[d][:, 0:NCH - 1], in_=dd[:, 0:NCH - 1], func=AF.Exp), reads=[dd], writes=[dmul[d]])
                    kb.op(kb.DVE, lambda e: e.tensor_copy(out=d0t[d][:], in_=dmul[d][:]), reads=[dmul[d]], writes=[d0t[d]])
                    kb.op(kb.DVE, lambda e: e.memset(d0t[d][:, 0:1], 0.0), writes=[d0t[d]])
                    kb.op(kb.DVE, lambda e: e.tensor_tensor(out=c3, in0=c3, in1=ridx[:].unsqueeze(2).to_broadcast([128, NCH, CH]), op=ALU.subtract),
                          reads=bC_T + [ridx], writes=bC_T)
                    kb.op(kb.ACT, lambda e: e.activation(out=bB_t[:], in_=bC_t[:], func=AF.Exp), reads=bC_T, writes=bB_T)
                    kb.op(kb.DVE, lambda e: e.tensor_tensor(out=qd[d][:], in0=qS_t[:], in1=bB_t[:], op=ALU.mult), reads=qS_T + bB_T, writes=[qd[d]])
                    kb.op(kb.ACT, lambda e: e.activation(out=bC_t[:], in_=bC_t[:], func=AF.Exp, scale=-1.0), reads=bC_T, writes=bC_T)
                    kb.op(kb.DVE, lambda e: e.tensor_tensor(out=kd[d][:], in0=bA[:], in1=bC_t[:], op=ALU.mult), reads=bA_T[d] + bC_T, writes=[kd[d]])
                if getattr(self, 'hg_stop', 99) <= 3:
                    return
                for d in range(2):
                    data0m, data1, SS = (bC_t, bA_t[0], bB_t) if d == 0 else (bC_t, bA_t[1], qS_t)
                    data0T, data1T, SST = (bC_T, bA_T[0], bB_T) if d == 0 else (bC_T, bA_T[1], qS_T)
                    for (g0, g1) in SG8:
                        pt = pr.next()
                        pv = self.psbf(pt)
                        kb.tr(pt, [(pv[0:CH, j * 128:(j + 1) * 128], kd[d][:, (g0 + j) * CH:(g0 + j + 1) * CH]) for j in range(8)], self.ident, reads=[kd[d]])
                        self.copy(kb.ACT, ktm[:, g0:g1, :], pv[0:CH, :].rearrange("p (j k) -> p j k", k=128), [pt], [ktm])
                    if getattr(self, 'hg_stop', 99) <= 4:
                        return
                    kb.op(kb.DVE, lambda e: e.tensor_copy(out=data0m[:].rearrange("p (v i) -> p v i", i=NCH), in_=d0t[d][:].unsqueeze(1).to_broadcast([128, HV, NCH])),
                          reads=[d0t[d]], writes=data0T)
                    for vh in range(2):
                        d1v = data1[:].rearrange("p (v i) -> p i v", i=NCH)
                        for (i0, i1) in SG8:
                            pt = pr.next()
                            grp = []
                            for j in range(8):
                                c = chunk_of(d, i0 + j)
                                grp.append((pt[:, j * 64:(j + 1) * 64], [(ktm[:, c, :], Vc[:, c, :])]))
                            kb.mm_multi(pt, grp, [ktm] + VcT)
                            kb.op(kb.DVE, lambda e: e.tensor_tensor(out=d1v[:, i0:i1, :], in0=pt[:, :].rearrange("p (j v) -> p j v", v=64)[:, :, vh * HV:(vh + 1) * HV],
                                                                    in1=dmul[d][:, i0:i1].unsqueeze(2).to_broadcast([128, 8, HV]), op=ALU.mult),
                                  reads=[pt, dmul[d]], writes=data1T)
                        kb.op(kb.DVE, lambda e: e.tensor_tensor_scan(out=SS[:], data0=data0m[:], data1=data1[:], initial=0.0, op0=ALU.mult, op1=ALU.add),
                              reads=data0T + data1T, writes=SST)
                        kb.op(kb.DVE, lambda e: e.tensor_copy(out=Sbf[d][:, 1:NCH, vh * HV:(vh + 1) * HV], in_=SS[:].rearrange("p (v i) -> p i v", i=NCH)[:, 0:NCH - 1, :]),
                              reads=SST, writes=[Sbf[d]])
                    if getattr(self, 'hg_stop', 99) <= 5:
                        return
                    pA, pB = (pat[0], pat[1]) if d == 0 else (pat[1], pat[0])
                    for dst, src, pp in ((kA, kd[d], pA), (kB, kd[d], pB), (qB, qd[d], pB)):
                        kb.op(kb.DVE, lambda e: e.tensor_tensor(out=v3(dst[:]), in0=v3(src[:]), in1=pp[:].unsqueeze(1).to_broadcast([128, NCH, CH]), op=ALU.mult),
                              reads=[src, pp], writes=[dst])
                    for (i0, i1) in SG16:
                        n = i1 - i0
                        cs = [chunk_of(d, i0 + j) for j in range(n)]
                        pa = pr.next()
                        kb.mm_multi(pa, [(pa[0:CH, j * CH:(j + 1) * CH], [(kA[:, c * CH:(c + 1) * CH], qd[d][:, c * CH:(c + 1) * CH]),
                                                                         (kB[:, c * CH:(c + 1) * CH], qB[:, c * CH:(c + 1) * CH])]) for j, c in enumerate(cs)],
                                    [kA, kB, qB, qd[d]])
                        AT = ATr.next()
                        kb.op(kb.DVE, lambda e: e.tensor_tensor(out=AT[:, 0:n, :], in0=pa[0:CH, 0:n * CH].rearrange("p (j t) -> p j t", t=CH),
                                                                in1=mask[d][:].unsqueeze(1).to_broadcast([CH, n, CH]), op=ALU.mult),
                              reads=[pa, mask[d]], writes=[AT])
                        po = pr.next()
                        grp = []
                        for j, c in enumerate(cs):
                            grp.append((po[0:64, j * CH:(j + 1) * CH], [(Vc[:, c, :], AT[:, j, :]), (Sbf[d][:, i0 + j, :], qd[d][:, c * CH:(c + 1) * CH])]))
                        kb.mm_multi(po, grp, [AT, qd[d], Sbf[d]] + VcT)
                        clo, chi = min(cs), max(cs)
                        ov = oT[d][:, clo * CH:(chi + 1) * CH].rearrange("p (j t) -> p j t", t=CH)
                        if d == 1:
                            ov = ov[:, ::-1, :]
                        self.copy(kb.ACT, ov, po[0:64, 0:n * CH].rearrange("p (j t) -> p j t", t=CH), [po], [oT[d]])
                if getattr(self, 'hg_stop', 99) <= 6:
                    return
                kb.op(kb.DVE, lambda e: e.tensor_tensor(out=oT[0][:], in0=oT[0][:], in1=oT[1][:], op=ALU.add), reads=[oT[0], oT[1]], writes=[oT[0]])
                sqb = bB_t[0:64, :].bitcast(BF16)
                kb.op(kb.ACT, lambda e: e.activation(out=sqb[:, 0:NTOK], in_=oT[0][:], func=AF.Square), reads=[oT[0]], writes=bB_T)
                for bi, (s0, wd) in enumerate(TBLK):
                    pt = pr.next()
                    kb.mm(pt, pt[0:64, 0:wd], [(self.ones[0:64, 0:64], sqb[:, s0:s0 + wd])], reads=[self.ones] + bB_T)
                    rs = rsr.next()
                    self.rstd(rs[:, 0, 0:wd], pt[0:64, 0:wd], rs[:, 1, 0:wd], 1.0 / 64, [rs], [pt])
                    kb.op(kb.DVE, lambda e: e.scalar_tensor_tensor(out=rs[:, 1, 0:wd], in0=oT[0][:, s0:s0 + wd], scalar=gcol[:, h:h + 1], in1=rs[:, 0, 0:wd],
                                                                   op0=ALU.mult, op1=ALU.mult), reads=[oT[0], gcol, rs], writes=[rs])
                    kb.op(kb.DVE, lambda e: e.tensor_tensor(out=hgo[:, s0:s0 + wd], in0=rs[:, 1, 0:wd], in1=zgT[:, s0:s0 + wd], op=ALU.mult),
                          reads=[rs, zgT], writes=[hgo])
                if getattr(self, 'hg_stop', 99) <= 7:
                    return
                mt = self.mixT[4 + h // 2]
                kb.dma(kb.SP, "mixo", mt.ap[(h % 2) * 64:(h % 2 + 1) * 64, :], hgo[:], reads=[hgo], writes=[mt])
                if getattr(self, 'hg_stop', 99) <= 8 + h:
                    return

    def phase_FN(self, l, b, xmodT, xmT, last):
        kb = self.kb
        W = self.W[l]
        with kb.scope():
            WF = kb.sbT("f_WF", [128, 8, 256], BF16)
            BD = kb.sbT("f_BD", [128, 2, 512], BF16)
            WFO = kb.sbT("f_WFO", [128, 2, 256], BF16)
            self.wload(WF, W["WF"], 8)
            self.wload(WFO, W["WFO"], 2)
            kb.dma(kb.SP, "w", BD[:], self.C["bd"].rearrange("(kc p) n -> p kc n", p=128), writes=[BD])
            tcc = kb.sbT("f_tcc", [128, 2, LC], BF16)
            tsc = kb.sbT("f_tsc", [128, 2, LC], BF16)
            kb.dma(kb.SP, "w", tcc[:], self.C["dftc_c"].rearrange("(kc p) n -> p kc n", p=128), writes=[tcc])
            kb.dma(kb.SP, "w", tsc[:], self.C["dfts_c"].rearrange("(kc p) n -> p kc n", p=128), writes=[tsc])
            zT = kb.sb("f_zT", [128, 2, NTOK], BF16)
            zTT = [T(zT, f"zT{i}") for i in range(5)]
            zcs = kb.sb("f_zcs", [128, NT, 512], BF16)
            zcT = [T(zcs, f"zc{i}") for i in range(NT)]
            mx = kb.sb("f_mx", [128, 2, NTOK], BF16)
            mxT = [T(mx, f"mx{i}") for i in range(5)]
            tabC = Rot([kb.sbT(f"f_tC{i}", [128, 16, 512], BF16) for i in range(2)])
            tabS = Rot([kb.sbT(f"f_tS{i}", [128, 16, 512], BF16) for i in range(2)])
            fo = Rot([kb.sbT(f"f_fo{i}", [128, 512], BF16) for i in range(2)])
            pr = self.prs(0, 8)
            for bi, (s0, wd) in enumerate(TBLK):
                tl = [0, 1] if bi == 0 else list(range(2 + 4 * (bi - 1), 6 + 4 * (bi - 1)))
                for m in range(2):
                    pt = pr.next()
                    kb.mm(pt, pt[:, 0:wd], [(WF[:, k, m * 128:(m + 1) * 128], xmodT[:, k, s0:s0 + wd]) for k in range(8)], reads=[WF] + [xmT[i] for i in tl])
                    self.copy(kb.ACT, zT[:, m, s0:s0 + wd], pt[:, 0:wd], [pt], [zTT[bi]])
            for i in range(NT):
                bi = 0 if i < 2 else 1 + (i - 2) // 4
                pt = pr.next()
                kb.mm(pt, pt[:, :], [(zT[:, m, i * 128:(i + 1) * 128], BD[:, m, :]) for m in range(2)], reads=[BD, zTT[bi]])
                self.copy(kb.DVE, zcs[:, i, :], pt[:, :], [pt], [zcT[i]])
            for m in range(2):
                pt = pr.next()
                prs_ = [(zcs[:, tt, m * 128:(m + 1) * 128], tcc[:, tt, :]) for tt in range(2)] + [(zcs[:, tt, 256 + m * 128:256 + (m + 1) * 128], tsc[:, tt, :]) for tt in range(2)]
                kb.mm(pt, pt[:, 0:LC], prs_, reads=[tcc, tsc, zcT[0], zcT[1]])
                self.copy(kb.ACT, mx[:, m, 0:LC], pt[:, 0:LC], [pt], [mxT[0]])
            cv = self.C["dftc"].rearrange("(tt p) k -> p tt k", p=128)
            sv = self.C["dfts"].rearrange("(tt p) k -> p tt k", p=128)
            for k4 in range(4):
                tc_, ts_ = tabC.next(), tabS.next()
                kb.dma(kb.SP, "tab", tc_[:], cv[:, :, k4 * 512:(k4 + 1) * 512], writes=[tc_])
                kb.dma(kb.SP, "tab", ts_[:], sv[:, :, k4 * 512:(k4 + 1) * 512], writes=[ts_])
                for m in range(2):
                    pt = pr.next()
                    prs_ = [(zcs[:, 2 + tt, m * 128:(m + 1) * 128], tc_[:, tt, :]) for tt in range(16)] + \
                           [(zcs[:, 2 + tt, 256 + m * 128:256 + (m + 1) * 128], ts_[:, tt, :]) for tt in range(16)]
                    kb.mm(pt, pt[:, :], prs_, reads=[tc_, ts_] + zcT[2:])
                    self.copy(kb.ACT, mx[:, m, LC + k4 * 512:LC + (k4 + 1) * 512], pt[:, :], [pt], [mxT[1 + k4]])
            for bi, (s0, wd) in enumerate(TBLK):
                for m2 in range(2):
                    pt = pr.next()
                    kb.mm(pt, pt[:, 0:wd], [(WFO[:, m, m2 * 128:(m2 + 1) * 128], mx[:, m, s0:s0 + wd]) for m in range(2)], reads=[WFO, mxT[bi]])
                    f = fo.next()
                    self.copy(kb.DVE, f[:, 0:wd], pt[:, 0:wd], [pt], [f])
                    kb.dma(kb.SP, "mixo", self.mixT[6 + m2].ap[:, s0:s0 + wd], f[:, 0:wd], reads=[f], writes=[self.mixT[6 + m2]])

    def phase_EG(self, l, b, last):
        kb = self.kb
        W = self.W[l]
        with kb.scope():
            WO = kb.sbT("e_WO", [128, 8, D], BF16)
            self.wload(WO, W["WO"], 8)
            bc = {}
            for nm, m, slot in [("G1l", b, 2), ("S2l", b, 4), ("H2l", b, 3), ("G2l", b, 5)] + ([] if last else [("G1c", 4, 2), ("S2c", 4, 4), ("H2c", 4, 3), ("G2c", 4, 5)]):
                bc[nm] = kb.sbT("e_" + nm, [128, D], F32)
                self.load_bc(bc[nm], l, m, slot)
            if last:
                gF = kb.sbT("e_gF", [128, D], F32)
                kb.dma(kb.SP, "w", gF[:], self.I["final_norm_g"].rearrange("(o n) -> o n", o=1).partition_broadcast(128), writes=[gF])
            r = self.norm_res("e_")
            xin = Rot([kb.sbT(f"e_xin{i}", [128, D], F32) for i in range(2)])
            ytmp = Rot([kb.sbT(f"e_yt{i}", [128, 512], F32) for i in range(2)])
            sgr = Rot([kb.sbT(f"e_sg{i}", [128, 512], F32) for i in range(2)])
            x1 = kb.sb("e_x1", [128, 6, D], F32)
            x1T = [T(x1, f"x1_{i}") for i in range(6)]
            xm2 = kb.sb("e_xm2", [128, 8, 768], BF16)
            xm2T = [T(xm2, f"xm2_{i}") for i in range(6)]
            actT = kb.sb("e_act", [128, NF, 768], BF16)
            actTT = [T(actT, f"act{i}") for i in range(NF)]
            mixb = kb.sbT("e_mixb", [128, 8, 768], BF16)
            wgu = Rot([kb.sbT(f"e_wgu{i}", [128, 8, 512], BF16) for i in range(2)])
            wdn = Rot([kb.sbT(f"e_wdn{i}", [128, NF, 512], BF16) for i in range(1)])
            if last:
                fin = Rot([kb.sbT(f"e_fin{i}", [128, D], F32) for i in range(2)])
                fst = Rot([kb.sbT(f"e_fst{i}", [128, 4], F32) for i in range(2)])
            guv = W["WGU"].ap.rearrange("(kc p) n -> p kc n", p=128)
            dnv = W["WDN"].ap.rearrange("(kc p) n -> p kc n", p=128)
            pr = self.prs(0, 8)
            for blk in range(3):
                tiles = [i for i in range(6 * blk, 6 * blk + 6) if not (last and i < 2)]
                nt_ = len(tiles)
                t0 = tiles[0] * 128
                ntk = nt_ * 128
                subs = [(o, min(512, ntk - o)) for o in range(0, ntk, 512)]
                for c in range(8):
                    kb.dma(kb.SP, "mixi", mixb[:, c, 0:ntk], self.mixT[c].ap[:, t0:t0 + ntk], reads=[self.mixT[c]], writes=[mixb])
                for ti, i in enumerate(tiles):
                    xt = xin.next()
                    src, srcT = self.x_src(l, b, i)
                    kb.dma(kb.SP, "xin", xt[:], src, reads=[srcT] if srcT else [], writes=[xt])
                    G1 = bc["G1c"] if i < 2 else bc["G1l"]
                    for hf in range(2):
                        pt = pr.next()
                        kb.mm(pt, pt[:, :], [(mixb[:, c, ti * 128:(ti + 1) * 128], WO[:, c, hf * 512:(hf + 1) * 512]) for c in range(8)], reads=[mixb, WO])
                        yt = ytmp.next()
                        kb.op(kb.DVE, lambda e: e.tensor_tensor(out=yt[:], in0=pt[:, :], in1=G1[:, hf * 512:(hf + 1) * 512], op=ALU.mult), reads=[pt, G1], writes=[yt])
                        kb.op(kb.DVE, lambda e: e.tensor_tensor(out=x1[:, ti, hf * 512:(hf + 1) * 512], in0=xt[:, hf * 512:(hf + 1) * 512], in1=yt[:], op=ALU.add),
                              reads=[xt, yt], writes=[x1T[ti]])
                for ti, i in enumerate(tiles):
                    S2, H2 = (bc["S2c"], bc["H2c"]) if i < 2 else (bc["S2l"], bc["H2l"])
                    self.norm_tile(r, T_view(x1T[ti], x1[:, ti, :]), S2, H2, xm2T[ti], xm2[:, :, ti * 128:(ti + 1) * 128])
                for fg in range(NF // 2):
                    w = wgu.next()
                    kb.dma(kb.SP, "wg", w[:, :, 0:256], guv[:, :, fg * 256:(fg + 1) * 256], reads=[W["WGU"]], writes=[w])
                    kb.dma(kb.SP, "wg", w[:, :, 256:512], guv[:, :, FFH + fg * 256:FFH + (fg + 1) * 256], reads=[W["WGU"]], writes=[w])
                    for fc in range(2):
                        f = fg * 2 + fc
                        for (o, wd) in subs:
                            xr = xm2T[o // 128:(o + wd) // 128]
                            pg, pu = pr.next(), pr.next()
                            kb.mm(pg, pg[:, 0:wd], [(w[:, k, fc * 128:(fc + 1) * 128], xm2[:, k, o:o + wd]) for k in range(8)], reads=[w] + xr)
                            kb.mm(pu, pu[:, 0:wd], [(w[:, k, 256 + fc * 128:256 + (fc + 1) * 128], xm2[:, k, o:o + wd]) for k in range(8)], reads=[w] + xr)
                            sg = sgr.next()
                            kb.op(kb.ACT, lambda e: e.activation(out=sg[:, 0:wd], in_=pg[:, 0:wd], func=AF.Silu), reads=[pg], writes=[sg])
                            kb.op(kb.DVE, lambda e: e.tensor_tensor(out=actT[:, f, o:o + wd], in0=pu[:, 0:wd], in1=sg[:, 0:wd], op=ALU.mult), reads=[pu, sg], writes=[actTT[f]])
                for hf in range(2):
                    w = wdn.next()
                    kb.dma(kb.SP, "wd", w[:], dnv[:, :, hf * 512:(hf + 1) * 512], reads=[W["WDN"]], writes=[w])
                    for ti, i in enumerate(tiles):
                        G2 = bc["G2c"] if i < 2 else bc["G2l"]
                        pt = pr.next()
                        kb.mm(pt, pt[:, :], [(actT[:, f, ti * 128:(ti + 1) * 128], w[:, f, :]) for f in range(NF)], reads=[w] + actTT)
                        yt = ytmp.next()
                        kb.op(kb.DVE, lambda e: e.tensor_tensor(out=yt[:], in0=pt[:, :], in1=G2[:, hf * 512:(hf + 1) * 512], op=ALU.mult), reads=[pt, G2], writes=[yt])
                        kb.op(kb.DVE, lambda e: e.tensor_tensor(out=x1[:, ti, hf * 512:(hf + 1) * 512], in0=x1[:, ti, hf * 512:(hf + 1) * 512], in1=yt[:], op=ALU.add),
                              reads=[x1T[ti], yt], writes=[x1T[ti]])
                for ti, i in enumerate(tiles):
                    if not last:
                        kb.dma(kb.SP, "xso", self.xs[b].ap[i * 128:(i + 1) * 128, :], x1[:, ti, :], reads=[x1T[ti]], writes=[self.xs[b]])
                    else:
                        st = fst.next()
                        fo_ = fin.next()
                        junk = r["junk"]
                        kb.op(kb.ACT, lambda e: e.activation(out=junk[:], in_=x1[:, ti, :], func=AF.Square, accum_out=st[:, 0:1]), reads=[x1T[ti]], writes=[junk, st])
                        self.rstd(st[:, 2:3], st[:, 0:1], st[:, 1:2], 1.0 / D, [st])
                        kb.op(kb.DVE, lambda e: e.scalar_tensor_tensor(out=fo_[:], in0=x1[:, ti, :], scalar=st[:, 2:3], in1=gF[:], op0=ALU.mult, op1=ALU.mult),
                              reads=[x1T[ti], st, gF], writes=[fo_])
                        kb.dma(kb.SP, "out", self.out[b, (i - 2) * 128:(i - 1) * 128, :], fo_[:], reads=[fo_], writes=[self.outT])

    def build(self):
        kb = self.kb
        self.convert_weights()
        self.phase_cact()
        for l in self.layers:
            self.phase_mod(l)
        for b in range(self.NB):
            for l in self.layers:
                last = (l == DEPTH - 1)
                with kb.scope():
                    xmodT = kb.sb("xmodT", [128, 8, NTOK], BF16)
                    xmT = [T(xmodT, f"xm{i}") for i in range(NT)]
                    self.phase_A(l, b, xmodT, xmT)
                    self.phase_MLA(l, b, xmodT, xmT, last)
                    self.phase_HG(l, b, xmodT, xmT, last)
                    self.phase_FN(l, b, xmodT, xmT, last)
                self.phase_EG(l, b, last)
        kb.finish()
        return kb.nc


class T_view:
    def __init__(self, base, ap):
        self.base = base
        self.ap = ap

    def __getitem__(self, k):
        return self.ap[k]

    w = property(lambda s: s.base.w, lambda s, v: setattr(s.base, "w", v))
    r = property(lambda s: s.base.r, lambda s, v: setattr(s.base, "r", v))
    excl = property(lambda s: s.base.excl)


_CACHE = {}
NB_PER_LAUNCH = 4


def kernel(**inputs):
    NBC = BATCH // NCORES
    NB = NB_PER_LAUNCH
    if "nc" not in _CACHE:
        _CACHE["nc"] = Prog(NB=NB, layers=(0, 1)).build()
        _CACHE["consts"] = make_consts()
    nc = _CACHE["nc"]
    consts = _CACHE["consts"]
    full = {name: np.asarray(inputs[name], dtype=np.float32) for name, _ in INPUT_SPECS}
    out = np.empty((BATCH, SEQ, D), dtype=np.float32)
    for j in range(0, NBC, NB):
        in_maps = []
        for i in range(NCORES):
            m = {}
            b0 = i * NBC + j
            for name, shape in INPUT_SPECS:
                a = full[name]
                if shape[0] is None:
                    a = a[b0:b0 + NB]
                m[name] = np.ascontiguousarray(a)
            for k, v in consts.items():
                m["k_" + k] = v
            in_maps.append(m)
        res = run_bass_kernel_spmd(nc, in_maps, core_ids=list(range(NCORES)))
        for i in range(NCORES):
            b0 = i * NBC + j
            out[b0:b0 + NB] = np.asarray(res.results[i]["out"], dtype=np.float32)
    return out
```

```python
import contextlib
import math
import numpy as np
import ml_dtypes
import concourse.bass as bass
import concourse.mybir as mybir
from concourse.bass_utils import run_bass_kernel_spmd

F32 = mybir.dt.float32
BF16 = mybir.dt.bfloat16
ALU = mybir.AluOpType
AF = mybir.ActivationFunctionType
AX = mybir.AxisListType

D = 1024
NCORES = 8
BATCH = 32
SEQ = 2048
LC = 256
NTOK = SEQ + LC
NT = NTOK // 128
CH = 32
NCH = NTOK // CH
DEPTH = 2
EPS = 1e-6
FFH = 2816
NF = FFH // 128
IN_WIDTH = 2752
O_CQ, O_CKV, O_KR, O_HQ, O_HFF, O_HFB, O_HI, O_HG, O_FN = 0, 256, 384, 448, 960, 1472, 1984, 2240, 2496
TBLK = [(0, 256), (256, 512), (768, 512), (1280, 512), (1792, 512)]


class T:
    __slots__ = ("ap", "w", "r", "name", "excl")

    def __init__(self, ap, name="", excl=False):
        self.ap = ap
        self.w = None
        self.r = {}
        self.name = name
        self.excl = excl

    def __getitem__(self, k):
        return self.ap[k]


class Eng:
    def __init__(self, kb, name, eng):
        self.name = name
        self.eng = eng
        self.sem = kb.new_sem("e_" + name)
        self.cnt = 0
        self.waited = {}


class KB:
    def __init__(self):
        self.nc = bass.Bass("TRN2", target_bir_lowering=False)
        self.es = contextlib.ExitStack()
        self.nsem = 0
        self.dsem = {}
        nc = self.nc
        self.PE = Eng(self, "pe", nc.tensor)
        self.ACT = Eng(self, "act", nc.scalar)
        self.DVE = Eng(self, "dve", nc.vector)
        self.POOL = Eng(self, "pool", nc.gpsimd)
        self.SP = Eng(self, "sp", nc.sync)
        self.engs = [self.PE, self.ACT, self.DVE, self.POOL, self.SP]
        self.ninst = 0
        self.scopes = []

    def new_sem(self, name):
        self.nsem += 1
        return self.es.enter_context(self.nc.semaphore(name))

    def _stack(self):
        return self.scopes[-1] if self.scopes else self.es

    def sb(self, name, shape, dt=F32):
        self.uid = getattr(self, "uid", 0) + 1
        return self._stack().enter_context(self.nc.sbuf_tensor(f"{name}_{self.uid}", list(shape), dt))

    def sbT(self, name, shape, dt=F32):
        t = self.sb(name, shape, dt)
        return T(t, name)

    def ps(self, name, shape, dt=F32):
        return self._stack().enter_context(self.nc.psum_tensor(name, list(shape), dt))

    def dram(self, name, shape, dt, kind=None):
        if kind is None:
            return self.nc.dram_tensor(name, list(shape), dt).ap()
        return self.nc.dram_tensor(name, list(shape), dt, kind=kind).ap()

    @contextlib.contextmanager
    def scope(self):
        st = contextlib.ExitStack()
        self.scopes.append(st)
        try:
            yield
        finally:
            self.barrier()
            self.scopes.pop()
            st.close()

    def _needs(self, reads, writes):
        needs = {}

        def add(ev):
            if ev is None:
                return
            key = id(ev[0])
            if key not in needs or needs[key][1] < ev[1]:
                needs[key] = ev
        for t in reads:
            add(t.w)
            if t.excl:
                for ev in t.r.values():
                    add(ev)
        for t in writes:
            add(t.w)
            for ev in t.r.values():
                add(ev)
        return needs

    def _sync(self, E, reads, writes):
        for key, (sem, val, grp) in self._needs(reads, writes).items():
            if sem is E.sem and E is self.PE:
                continue
            if grp is not None:
                val = self.dsem[grp][1]
            if E.waited.get(key, 0) >= val:
                continue
            E.eng.wait_ge(sem, val)
            E.waited[key] = val
            self.ninst += 1

    def _record(self, ev, reads, writes):
        key = id(ev[0])
        for t in reads:
            t.r[key] = ev
        for t in writes:
            t.w = ev
            t.r = {}

    def op(self, E, fn, reads=(), writes=()):
        self._sync(E, reads, writes)
        inst = fn(E.eng)
        E.cnt += 1
        inst.then_inc(E.sem, 1)
        self.ninst += 1
        self._record((E.sem, E.cnt, None), reads, writes)
        return inst

    def mm(self, pst, out_ap, pairs, reads, start=True, stop=True):
        E = self.PE
        self._sync(E, reads, [pst])
        n = len(pairs)
        inst = None
        for i, (l, r) in enumerate(pairs):
            inst = self.nc.tensor.matmul(out_ap, lhsT=l, rhs=r, start=(start and i == 0), stop=(stop and i == n - 1))
            self.ninst += 1
        E.cnt += 1
        inst.then_inc(E.sem, 1)
        self._record((E.sem, E.cnt, None), reads, [pst])

    def mm_multi(self, pst, groups, reads):
        E = self.PE
        self._sync(E, reads, [pst])
        inst = None
        for out_ap, pairs in groups:
            n = len(pairs)
            for i, (l, r) in enumerate(pairs):
                inst = self.nc.tensor.matmul(out_ap, lhsT=l, rhs=r, start=(i == 0), stop=(i == n - 1))
                self.ninst += 1
        E.cnt += 1
        inst.then_inc(E.sem, 1)
        self._record((E.sem, E.cnt, None), reads, [pst])

    def tr(self, pst, outs_ins, ident, reads):
        E = self.PE
        self._sync(E, list(reads) + [ident], [pst])
        inst = None
        for o, i in outs_ins:
            inst = self.nc.tensor.transpose(o, i, ident.ap[0:i.shape[0], 0:i.shape[0]])
            self.ninst += 1
        E.cnt += 1
        inst.then_inc(E.sem, 1)
        self._record((E.sem, E.cnt, None), list(reads) + [ident], [pst])

    def dma(self, Q, grp, out, in_, reads=(), writes=(), slow=False):
        if grp not in self.dsem:
            self.dsem[grp] = [self.new_sem("d_" + grp), 0]
        self._sync(Q, reads, writes)
        if slow:
            inst = Q.eng.dma_start(out=out, in_=in_, allow_slow_non_contiguous=True)
        else:
            inst = Q.eng.dma_start(out=out, in_=in_)
        ds = self.dsem[grp]
        ds[1] += 16
        inst.then_inc(ds[0], 16)
        self.ninst += 1
        self._record((ds[0], ds[1], grp), reads, writes)

    def barrier(self):
        for E in self.engs:
            for O in self.engs:
                if O is E or O.cnt == 0:
                    continue
                key = id(O.sem)
                if E.waited.get(key, 0) < O.cnt:
                    E.eng.wait_ge(O.sem, O.cnt)
                    E.waited[key] = O.cnt
            for grp, (sem, cnt) in self.dsem.items():
                key = id(sem)
                if cnt and E.waited.get(key, 0) < cnt:
                    E.eng.wait_ge(sem, cnt)
                    E.waited[key] = cnt

    def finish(self):
        self.barrier()


class Rot:
    def __init__(self, items):
        self.items = items
        self.i = 0

    def next(self):
        t = self.items[self.i % len(self.items)]
        self.i += 1
        return t


def _bf(a):
    return np.ascontiguousarray(a.astype(np.float32)).astype(ml_dtypes.bfloat16)


def make_consts():
    c = {}
    c["ident"] = _bf(np.eye(128))
    c["ones"] = _bf(np.ones((128, 128)))
    rows = SEQ // 64
    row_pos = np.repeat(np.arange(rows, dtype=np.float32), 64)
    col_pos = np.tile(np.arange(64, dtype=np.float32), rows)
    inv_freq = (np.float32(10000.0) ** (-np.arange(0, 32, 2, dtype=np.float32) / np.float32(32))).astype(np.float32)
    ang_r = row_pos[:, None] * inv_freq
    ang_c = col_pos[:, None] * inv_freq
    ang = np.concatenate([ang_r, ang_r, ang_c, ang_c], axis=-1).astype(np.float32)
    cos = np.cos(ang).astype(np.float32).T
    sin = np.sin(ang).astype(np.float32).T.copy()
    sin[0:16] *= -1.0
    sin[32:48] *= -1.0
    c["cos"] = np.ascontiguousarray(cos)
    c["sin"] = np.ascontiguousarray(sin)
    def dft(n, scale):
        k = np.arange(n, dtype=np.int64)
        m = (k[:, None] * k[None, :]) % n
        a = 2.0 * np.pi * m.astype(np.float64) / n
        return np.cos(a) * scale, -np.sin(a) * scale
    ct, nst = dft(SEQ, 1.0 / math.sqrt(SEQ))
    c["dftc"] = _bf(ct)
    c["dfts"] = _bf(nst)
    cc, nsc = dft(LC, 1.0 / math.sqrt(LC))
    c["dftc_c"] = _bf(cc)
    c["dfts_c"] = _bf(nsc)
    c64, ns64 = dft(64, 1.0 / 8.0)
    bdc = np.zeros((256, 256))
    bds = np.zeros((256, 256))
    for g in range(4):
        bdc[g * 64:(g + 1) * 64, g * 64:(g + 1) * 64] = c64
        bds[g * 64:(g + 1) * 64, g * 64:(g + 1) * 64] = -ns64
    c["bd"] = _bf(np.concatenate([bdc, bds], axis=1))
    s = np.arange(CH)
    c["maskf"] = (s[:, None] <= s[None, :]).astype(np.float32)
    c["maskb"] = (s[:, None] >= s[None, :]).astype(np.float32)
    return c


CONST_SPECS = [("ident", [128, 128], BF16), ("ones", [128, 128], BF16), ("cos", [64, SEQ], F32), ("sin", [64, SEQ], F32),
               ("dftc", [SEQ, SEQ], BF16), ("dfts", [SEQ, SEQ], BF16), ("dftc_c", [LC, LC], BF16), ("dfts_c", [LC, LC], BF16),
               ("bd", [256, 512], BF16), ("maskf", [CH, CH], F32), ("maskb", [CH, CH], F32)]

INPUT_SPECS = [("x", [None, SEQ, D]), ("c", [None, D]), ("ctx", [None, LC, D]), ("c_ctx", [D]),
               ("w_mod", [DEPTH, D, 6 * D]), ("b_mod", [DEPTH, 6 * D]), ("norm1_g", [DEPTH, D]), ("norm2_g", [DEPTH, D]),
               ("w_in", [DEPTH, D, IN_WIDTH]), ("q_norm_g", [DEPTH, 256]), ("w_uq", [DEPTH, 256, 768]),
               ("kv_norm_g", [DEPTH, 128]), ("w_ukv", [DEPTH, 128, 1024]), ("lb_param", [DEPTH, 2, 512]),
               ("hg_norm_g", [DEPTH, 256]), ("w_fourier", [DEPTH, 256, 256]), ("w_out", [DEPTH, D, D]),
               ("w_gate_up", [DEPTH, D, 2 * FFH]), ("w_down", [DEPTH, FFH, D]), ("final_norm_g", [D])]


class Prog:
    def __init__(self, NB=4, layers=(0, 1), debug=None, stop_after=None):
        self.NB = NB
        self.layers = tuple(layers)
        self.debug = debug or {}
        self.stop_after = stop_after
        kb = self.kb = KB()
        self.I = {}
        for name, shape in INPUT_SPECS:
            shape = [NB if s is None else s for s in shape]
            self.I[name] = kb.dram(name, shape, F32, kind="ExternalInput")
        self.C = {name: kb.dram("k_" + name, shape, dt, kind="ExternalInput") for name, shape, dt in CONST_SPECS}
        self.out = kb.dram("out", [NB, SEQ, D], F32, kind="ExternalOutput")
        self.outT = T(self.out, "out")
        self.dbg = {}
        for name, (shape, dt) in self.debug.items():
            self.dbg[name] = kb.dram("dbg_" + name, shape, dt, kind="ExternalOutput")
        self.W = {}
        for l in range(DEPTH):
            w = {}
            for nm, shape in [("WA", [D, 512]), ("WH", [D, 2048]), ("WF", [D, 256]), ("WUQ", [256, 1024]),
                              ("WUKV", [128, 1024]), ("WFO", [256, 256]), ("WO", [D, D]), ("WGU", [D, 2 * FFH]), ("WDN", [FFH, D])]:
                w[nm] = T(kb.dram(f"s_{nm}{l}", shape, BF16), f"{nm}{l}")
            self.W[l] = w
        self.MOD = [T(kb.dram(f"s_mod{l}", [5, 6 * D], F32), f"mod{l}") for l in range(DEPTH)]
        self.xs = [T(kb.dram(f"s_xs{b}", [NTOK, D], F32), f"xs{b}") for b in range(NB)]
        self.mixT = [T(kb.dram(f"s_mix{c}", [128, NTOK], BF16), f"mix{c}") for c in range(8)]
        self.ident = kb.sbT("ident", [128, 128], BF16)
        self.ones = kb.sbT("ones", [128, 128], BF16)
        self.epsT = kb.sbT("epsT", [128, 1], F32)
        self.cact = kb.sbT("cact", [128, 5, 8], F32)
        self.lb = kb.sbT("lb", [128, 2, 8], F32)
        self.lb1 = kb.sbT("lb1", [128, 2, 8], F32)
        self.psb = [T(kb.ps(f"psb{i}", [128, 512], F32), f"psb{i}", excl=True) for i in range(8)]
        self.pr = Rot(self.psb)
        kb.dma(kb.SP, "const", self.ident[:], self.C["ident"][:, :], writes=[self.ident])
        kb.dma(kb.SP, "const", self.ones[:], self.C["ones"][:, :], writes=[self.ones])
        kb.op(kb.DVE, lambda e: e.memset(self.epsT[:], EPS), writes=[self.epsT])

    def psbf(self, pt):
        return pt.ap[:].bitcast(BF16)

    def rstd(self, out, ss, tmp, inv_n, Ts, rT=()):
        kb = self.kb
        rT = list(rT)
        kb.op(kb.DVE, lambda e: e.tensor_scalar(out=tmp, in0=ss, scalar1=inv_n, scalar2=EPS, op0=ALU.mult, op1=ALU.add), reads=rT + list(Ts), writes=Ts)
        kb.op(kb.ACT, lambda e: e.activation(out=tmp, in_=tmp, func=AF.Sqrt), reads=Ts, writes=Ts)
        kb.op(kb.DVE, lambda e: e.reciprocal(out=out, in_=tmp), reads=Ts, writes=Ts)

    def copy(self, E, out, in_, reads, writes):
        kb = self.kb
        if E is kb.ACT:
            kb.op(E, lambda e: e.activation(out=out, in_=in_, func=AF.Copy), reads=reads, writes=writes)
        else:
            kb.op(E, lambda e: e.tensor_copy(out=out, in_=in_), reads=reads, writes=writes)

    def prs(self, a, b):
        return Rot(self.psb[a:b])

    def wload(self, dstT, srcT, kc):
        kb = self.kb
        if kc == 1:
            kb.dma(kb.SP, "w", dstT[:], srcT.ap[:, :], reads=[srcT], writes=[dstT])
        else:
            kb.dma(kb.SP, "w", dstT[:], srcT.ap.rearrange("(kc p) n -> p kc n", p=128), reads=[srcT], writes=[dstT])

    def tap(self, name, src_ap, srcT):
        if name in self.dbg:
            self.kb.dma(self.kb.SP, "dbg", self.dbg[name], src_ap, reads=[srcT])

    def convert_weights(self):
        kb = self.kb
        I = self.I
        with kb.scope():
            stg = Rot([kb.sbT(f"stg{i}", [128, 11264], BF16) for i in range(3)])

            def conv(dstT, src2d, K, pieces, cap):
                KC = K // 128
                srcv = src2d.rearrange("(kc p) n -> p kc n", p=128)
                dstv = dstT.ap.rearrange("(kc p) n -> p kc n", p=128)
                groups, cur, tot = [], [], 0
                for (dc, sc, n) in pieces:
                    if tot + n > cap:
                        groups.append(cur)
                        cur, tot = [], 0
                    cur.append((dc, sc, n))
                    tot += n
                if cur:
                    groups.append(cur)
                for g in groups:
                    st = stg.next()
                    tot = sum(n for _, _, n in g)
                    sv = st.ap[:, 0:KC * tot].rearrange("p (kc n) -> p kc n", kc=KC)
                    off = 0
                    for (dc, sc, n) in g:
                        kb.dma(kb.POOL, "cv_in", sv[:, :, off:off + n], srcv[:, :, sc:sc + n], writes=[st])
                        off += n
                    off = 0
                    for (dc, sc, n) in g:
                        kb.dma(kb.SP, "cv_out", dstv[:, :, dc:dc + n], sv[:, :, off:off + n], reads=[st], writes=[dstT])
                        off += n

            def split(total, step, d0=0, s0=0):
                return [(d0 + o, s0 + o, min(step, total - o)) for o in range(0, total, step)]

            rot = [(0, 16, 16), (16, 0, 16), (32, 48, 16), (48, 32, 16)]
            for l in self.layers:
                W = self.W[l]
                win = I["w_in"][l]
                conv(W["WA"], win, D, [(0, 0, 448)] + [(448 + d, O_KR + s, n) for d, s, n in rot], 1024)
                pcs = []
                for h in range(4):
                    pcs += [(h * 512, O_HQ + h * 128, 128), (h * 512 + 128, O_HFF + h * 128, 128), (h * 512 + 256, O_HFB + h * 128, 128),
                            (h * 512 + 384, O_HG + h * 64, 64), (h * 512 + 448, O_HI + h * 64, 64)]
                conv(W["WH"], win, D, pcs, 1024)
                conv(W["WF"], win, D, [(0, O_FN, 256)], 1024)
                pcs = []
                for h in range(4):
                    pcs.append((h * 256, h * 192, 192))
                    pcs += [(h * 256 + 192 + d, h * 192 + 128 + s, n) for d, s, n in rot]
                conv(W["WUQ"], I["w_uq"][l], 256, pcs, 2048)
                pcs = []
                for h in range(4):
                    pcs += [(h * 128, h * 256, 128), (512 + h * 128, h * 256 + 128, 128)]
                conv(W["WUKV"], I["w_ukv"][l], 128, pcs, 2048)
                conv(W["WFO"], I["w_fourier"][l], 256, [(0, 0, 256)], 2048)
                conv(W["WO"], I["w_out"][l], D, split(D, 1024), 1024)
                conv(W["WGU"], I["w_gate_up"][l], D, split(2 * FFH, 1024), 1024)
                conv(W["WDN"], I["w_down"][l], FFH, split(D, 512), 512)

    def phase_cact(self):
        kb = self.kb
        I = self.I
        for m in range(self.NB):
            kb.dma(kb.SP, "cact", self.cact[:, m, :], I["c"][m].rearrange("(p j) -> p j", j=8), writes=[self.cact])
        for m in range(self.NB, 5):
            kb.dma(kb.SP, "cact", self.cact[:, m, :], I["c_ctx"].rearrange("(p j) -> p j", j=8), writes=[self.cact])
        kb.op(kb.ACT, lambda e: e.activation(out=self.cact[:], in_=self.cact[:], func=AF.Silu), reads=[self.cact], writes=[self.cact])
        with kb.scope():
            p0 = kb.sbT("lbp0", [128, 8], F32)
            p1 = kb.sbT("lbp1", [128, 8], F32)
            for d in range(2):
                kb.dma(kb.SP, "cact", p0[:, d * 4:(d + 1) * 4], I["lb_param"][0, d].rearrange("(h p) -> p h", p=128), writes=[p0], slow=True)
                kb.dma(kb.SP, "cact", p1[:, d * 4:(d + 1) * 4], I["lb_param"][1, d].rearrange("(h p) -> p h", p=128), writes=[p1], slow=True)
            kb.op(kb.DVE, lambda e: e.tensor_tensor(out=p1[:], in0=p1[:], in1=p0[:], op=ALU.subtract), reads=[p0, p1], writes=[p1])
            kb.op(kb.ACT, lambda e: e.activation(out=self.lb1[:, 0, :], in_=p1[:], func=AF.Sigmoid), reads=[p1], writes=[self.lb1])
            kb.op(kb.DVE, lambda e: e.tensor_scalar(out=self.lb1[:, 1, :], in0=self.lb1[:, 0, :], scalar1=-1.0, scalar2=1.0, op0=ALU.mult, op1=ALU.add),
                  reads=[self.lb1], writes=[self.lb1])
            kb.op(kb.DVE, lambda e: e.memset(self.lb[:, 0, :], 0.0), writes=[self.lb])
            kb.op(kb.DVE, lambda e: e.memset(self.lb[:, 1, :], 1.0), writes=[self.lb])

    def phase_mod(self, l):
        kb = self.kb
        I = self.I
        with kb.scope():
            wb = Rot([kb.sbT(f"wm{i}", [128, 8, 512], F32) for i in range(2)])
            mod = kb.sbT("mod_sb", [5, 6 * D], F32)
            bm = kb.sbT("bm", [5, 6 * D], F32)
            g1 = kb.sbT("n1g5", [5, D], F32)
            g2 = kb.sbT("n2g5", [5, D], F32)
            kb.dma(kb.SP, "modm", bm[:], I["b_mod"][l:l + 1, :].partition_broadcast(5), writes=[bm])
            kb.dma(kb.SP, "modm", g1[:], I["norm1_g"][l:l + 1, :].partition_broadcast(5), writes=[g1])
            kb.dma(kb.SP, "modm", g2[:], I["norm2_g"][l:l + 1, :].partition_broadcast(5), writes=[g2])
            wv = I["w_mod"][l].rearrange("(p j) n -> p j n", j=8)
            for nb in range(12):
                w = wb.next()
                kb.dma(kb.SP, "modw", w[:], wv[:, :, nb * 512:(nb + 1) * 512], writes=[w])
                pt = self.pr.next()
                kb.mm(pt, pt[0:5, :], [(self.cact[:, :, j], w[:, j, :]) for j in range(8)], reads=[self.cact, w])
                kb.op(kb.DVE, lambda e: e.tensor_tensor(out=mod[:, nb * 512:(nb + 1) * 512], in0=pt[0:5, :], in1=bm[:, nb * 512:(nb + 1) * 512], op=ALU.add),
                      reads=[pt, bm], writes=[mod])
            for slot, g in ((1, g1), (4, g2)):
                sl = mod[:, slot * D:(slot + 1) * D]
                kb.op(kb.DVE, lambda e: e.scalar_tensor_tensor(out=sl, in0=sl, scalar=1.0, in1=g[:], op0=ALU.add, op1=ALU.mult),
                      reads=[mod, g], writes=[mod])
            kb.dma(kb.SP, "modo", self.MOD[l].ap[:, :], mod[:], reads=[mod], writes=[self.MOD[l]])
            self.tap(f"mod{l}", mod[:], mod)

    def load_bc(self, dstT, l, m, slot):
        self.kb.dma(self.kb.SP, "bc", dstT[:], self.MOD[l].ap[m:m + 1, slot * D:(slot + 1) * D].partition_broadcast(128),
                    reads=[self.MOD[l]], writes=[dstT])

    def norm_res(self, pfx):
        kb = self.kb
        r = {}
        r["junk"] = kb.sbT(pfx + "junk", [128, D], F32)
        r["st"] = Rot([kb.sbT(f"{pfx}st{i}", [128, 4], F32) for i in range(3)])
        r["tmp"] = Rot([kb.sbT(f"{pfx}tmp{i}", [128, D], F32) for i in range(2)])
        r["xm"] = Rot([kb.sbT(f"{pfx}xm{i}", [128, D], BF16) for i in range(2)])
        return r

    def norm_tile(self, r, xt, S, H, dstT, dst_ap):
        kb = self.kb
        st = r["st"].next()
        tmp = r["tmp"].next()
        xm = r["xm"].next()
        junk = r["junk"]
        kb.op(kb.ACT, lambda e: e.activation(out=junk[:], in_=xt[:], func=AF.Square, accum_out=st[:, 0:1]), reads=[xt], writes=[junk, st])
        self.rstd(st[:, 2:3], st[:, 0:1], st[:, 1:2], 1.0 / D, [st])
        kb.op(kb.DVE, lambda e: e.scalar_tensor_tensor(out=tmp[:], in0=xt[:], scalar=st[:, 2:3], in1=S[:], op0=ALU.mult, op1=ALU.mult),
              reads=[xt, st, S], writes=[tmp])
        kb.op(kb.DVE, lambda e: e.tensor_tensor(out=xm[:], in0=tmp[:], in1=H[:], op=ALU.add), reads=[tmp, H], writes=[xm])
        pt = self.pr.next()
        pv = self.psbf(pt)
        kb.tr(pt, [(pv[:, c * 128:(c + 1) * 128], xm[:, c * 128:(c + 1) * 128]) for c in range(8)], self.ident, reads=[xm])
        kb.op(kb.ACT, lambda e: e.activation(out=dst_ap, in_=pv.rearrange("p (c t) -> p c t", c=8), func=AF.Copy), reads=[pt], writes=[dstT])

    def x_src(self, l, b, i):
        if l == self.layers[0] and l == 0:
            if i < 2:
                return self.I["ctx"][b, i * 128:(i + 1) * 128, :], None
            return self.I["x"][b, (i - 2) * 128:(i - 1) * 128, :], None
        return self.xs[b].ap[i * 128:(i + 1) * 128, :], self.xs[b]

    def phase_A(self, l, b, xmodT, xmT):
        kb = self.kb
        with kb.scope():
            r = self.norm_res("a_")
            xin = Rot([kb.sbT(f"a_xin{i}", [128, D], F32) for i in range(3)])
            SH = [kb.sbT(f"a_sh{i}", [128, D], F32) for i in range(4)]
            self.load_bc(SH[0], l, b, 1)
            self.load_bc(SH[1], l, b, 0)
            self.load_bc(SH[2], l, 4, 1)
            self.load_bc(SH[3], l, 4, 0)
            for i in range(NT):
                xt = xin.next()
                src, srcT = self.x_src(l, b, i)
                kb.dma(kb.SP, "xin", xt[:], src, reads=[srcT] if srcT else [], writes=[xt])
                S, H = (SH[2], SH[3]) if i < 2 else (SH[0], SH[1])
                self.norm_tile(r, xt, S, H, xmT[i], xmodT[:, :, i * 128:(i + 1) * 128])

    def phase_MLA(self, l, b, xmodT, xmT, last):
        kb = self.kb
        W = self.W[l]
        I = self.I
        SC = 1.0 / math.sqrt(192.0)

        def blk_tiles(bi):
            return [0, 1] if bi == 0 else list(range(2 + 4 * (bi - 1), 6 + 4 * (bi - 1)))

        def tile_blk(j):
            return 0 if j < 2 else 1 + (j - 2) // 4
        with kb.scope():
            WA = kb.sbT("m_WA", [128, 8, 512], BF16)
            WUQ = kb.sbT("m_WUQ", [128, 2, 1024], BF16)
            WUKV = kb.sbT("m_WUKV", [128, 1024], BF16)
            self.wload(WA, W["WA"], 8)
            self.wload(WUQ, W["WUQ"], 2)
            self.wload(WUKV, W["WUKV"], 1)
            gq = kb.sbT("m_gq", [128, 2], F32)
            gkv = kb.sbT("m_gkv", [128, 1], F32)
            kb.dma(kb.SP, "w", gq[:], I["q_norm_g"][l].rearrange("(c p) -> p c", p=128), writes=[gq], slow=True)
            kb.dma(kb.SP, "w", gkv[:], I["kv_norm_g"][l].rearrange("(c p) -> p c", p=128), writes=[gkv], slow=True)
            cos = kb.sbT("m_cos", [64, SEQ], F32)
            sin = kb.sbT("m_sin", [64, SEQ], F32)
            kb.dma(kb.SP, "w", cos[:], self.C["cos"][:, :], writes=[cos])
            kb.dma(kb.SP, "w", sin[:], self.C["sin"][:, :], writes=[sin])
            cqnT = kb.sb("m_cqnT", [128, 2, NTOK], BF16)
            ckvnT = kb.sb("m_ckvnT", [128, NTOK], BF16)
            kpeT = kb.sb("m_kpeT", [64, NTOK], BF16)
            knT = kb.sb("m_knT", [128, 4, NTOK], BF16)
            Vall = kb.sb("m_V", [128, NT, 512], BF16)
            cqT = [T(cqnT, f"cq{i}") for i in range(5)]
            ckvT = [T(ckvnT, f"ckv{i}") for i in range(5)]
            kpT = [T(kpeT, f"kp{i}") for i in range(5)]
            knTT = [T(knT, f"kn{i}") for i in range(5)]
            VT = [T(Vall, f"V{i}") for i in range(NT)]
            sq = Rot([kb.sbT(f"m_sq{i}", [128, 512], BF16) for i in range(3)])
            rq = kb.sbT("m_rq", [128, 512], F32)
            rqt = kb.sbT("m_rqt", [128, 512], F32)
            rkv = kb.sbT("m_rkv", [128, 512], F32)
            rkvt = kb.sbT("m_rkvt", [128, 512], F32)
            ra = Rot([kb.sbT(f"m_ra{i}", [64, 512], F32) for i in range(2)])
            rb = Rot([kb.sbT(f"m_rb{i}", [64, 512], F32) for i in range(2)])
            pr = self.prs(0, 8)

            def rope(dst_ap, dstT, pa, pb, pos0, wd):
                a = ra.next()
                bb = rb.next()
                kb.op(kb.DVE, lambda e: e.tensor_tensor(out=a[:, 0:wd], in0=pa[0:64, 0:wd], in1=cos[:, pos0:pos0 + wd], op=ALU.mult), reads=[pa, cos], writes=[a])
                kb.op(kb.DVE, lambda e: e.tensor_tensor(out=bb[:, 0:wd], in0=pb[0:64, 0:wd], in1=sin[:, pos0:pos0 + wd], op=ALU.mult), reads=[pb, sin], writes=[bb])
                kb.op(kb.DVE, lambda e: e.tensor_tensor(out=dst_ap, in0=a[:, 0:wd], in1=bb[:, 0:wd], op=ALU.add), reads=[a, bb], writes=[dstT])

            for bi, (s0, wd) in enumerate(TBLK):
                xr = [xmT[i] for i in blk_tiles(bi)]
                pts = [pr.next() for _ in range(5)]
                for gi, (c0, m) in enumerate([(0, 128), (128, 128), (256, 128), (384, 64), (448, 64)]):
                    kb.mm(pts[gi], pts[gi][0:m, 0:wd], [(WA[:, c, c0:c0 + m], xmodT[:, c, s0:s0 + wd]) for c in range(8)], reads=[WA] + xr)
                sqs = []
                for gi in range(3):
                    q = sq.next()
                    kb.op(kb.ACT, lambda e: e.activation(out=q[:, 0:wd], in_=pts[gi][:, 0:wd], func=AF.Square), reads=[pts[gi]], writes=[q])
                    sqs.append(q)
                ssq = pr.next()
                sskv = pr.next()
                kb.mm(ssq, ssq[:, 0:wd], [(self.ones[:], sqs[0][:, 0:wd]), (self.ones[:], sqs[1][:, 0:wd])], reads=[self.ones, sqs[0], sqs[1]])
                kb.mm(sskv, sskv[:, 0:wd], [(self.ones[:], sqs[2][:, 0:wd])], reads=[self.ones, sqs[2]])
                self.rstd(rq[:, 0:wd], ssq[:, 0:wd], rqt[:, 0:wd], 1.0 / 256, [rq, rqt], [ssq])
                self.rstd(rkv[:, 0:wd], sskv[:, 0:wd], rkvt[:, 0:wd], 1.0 / 128, [rkv, rkvt], [sskv])
                for c in range(2):
                    kb.op(kb.DVE, lambda e: e.scalar_tensor_tensor(out=cqnT[:, c, s0:s0 + wd], in0=pts[c][:, 0:wd], scalar=gq[:, c:c + 1], in1=rq[:, 0:wd],
                                                                   op0=ALU.mult, op1=ALU.mult), reads=[pts[c], gq, rq], writes=[cqT[bi]])
                kb.op(kb.DVE, lambda e: e.scalar_tensor_tensor(out=ckvnT[:, s0:s0 + wd], in0=pts[2][:, 0:wd], scalar=gkv[:, 0:1], in1=rkv[:, 0:wd],
                                                               op0=ALU.mult, op1=ALU.mult), reads=[pts[2], gkv, rkv], writes=[ckvT[bi]])
                if bi == 0:
                    self.copy(kb.ACT, kpeT[:, s0:s0 + wd], pts[3][0:64, 0:wd], [pts[3]], [kpT[bi]])
                else:
                    rope(kpeT[:, s0:s0 + wd], kpT[bi], pts[3], pts[4], s0 - LC, wd)
            for bi, (s0, wd) in enumerate(TBLK):
                for h in range(4):
                    pt = pr.next()
                    kb.mm(pt, pt[:, 0:wd], [(WUKV[:, h * 128:(h + 1) * 128], ckvnT[:, s0:s0 + wd])], reads=[WUKV, ckvT[bi]])
                    self.copy(kb.ACT, knT[:, h, s0:s0 + wd], pt[:, 0:wd], [pt], [knTT[bi]])
            for i in range(NT):
                pt = pr.next()
                kb.mm(pt, pt[:, :], [(ckvnT[:, i * 128:(i + 1) * 128], WUKV[:, 512:1024])], reads=[WUKV, ckvT[tile_blk(i)]])
                self.copy(kb.DVE, Vall[:, i, :], pt[:, :], [pt], [VT[i]])
            qn_b = [kb.sb(f"m_qn{i}", [128, NTOK], BF16) for i in range(2)]
            qp_b = [kb.sb(f"m_qp{i}", [64, NTOK], BF16) for i in range(2)]
            qnTs = [[T(qn_b[i], f"qn{i}_{j}") for j in range(5)] for i in range(2)]
            qpTs = [[T(qp_b[i], f"qp{i}_{j}") for j in range(5)] for i in range(2)]
            pTr = Rot([kb.sbT(f"m_pT{i}", [128, 512], BF16) for i in range(3)])
            rden = Rot([kb.sbT(f"m_rden{i}", [128, 512], F32) for i in range(2)])
            oT = Rot([kb.sbT(f"m_oT{i}", [128, 512], BF16) for i in range(2)])
            Sr = self.prs(0, 4)
            Or = self.prs(4, 6)
            Dr = self.prs(6, 8)
            qblocks = [1, 2, 3, 4] if last else [0, 1, 2, 3, 4]
            for h in range(4):
                qn, qp = qn_b[h % 2], qp_b[h % 2]
                qnT, qpT = qnTs[h % 2], qpTs[h % 2]
                for bi in qblocks:
                    s0, wd = TBLK[bi]
                    pt = Sr.next()
                    kb.mm(pt, pt[:, 0:wd], [(WUQ[:, c, h * 256:h * 256 + 128], cqnT[:, c, s0:s0 + wd]) for c in range(2)], reads=[WUQ, cqT[bi]])
                    self.copy(kb.ACT, qn[:, s0:s0 + wd], pt[:, 0:wd], [pt], [qnT[bi]])
                    pa = Sr.next()
                    kb.mm(pa, pa[0:64, 0:wd], [(WUQ[:, c, h * 256 + 128:h * 256 + 192], cqnT[:, c, s0:s0 + wd]) for c in range(2)], reads=[WUQ, cqT[bi]])
                    if bi == 0:
                        self.copy(kb.ACT, qp[:, s0:s0 + wd], pa[0:64, 0:wd], [pa], [qpT[bi]])
                    else:
                        pb = Sr.next()
                        kb.mm(pb, pb[0:64, 0:wd], [(WUQ[:, c, h * 256 + 192:h * 256 + 256], cqnT[:, c, s0:s0 + wd]) for c in range(2)], reads=[WUQ, cqT[bi]])
                        rope(qp[:, s0:s0 + wd], qpT[bi], pa, pb, s0 - LC, wd)
                for bi in qblocks:
                    s0, wd = TBLK[bi]
                    keys = [0, 1] if bi == 0 else list(range(NT))
                    po = Or.next()
                    pd = Dr.next()
                    for ji, j in enumerate(keys):
                        sT = Sr.next()
                        kb.mm(sT, sT[:, 0:wd], [(knT[:, h, j * 128:(j + 1) * 128], qn[:, s0:s0 + wd]), (kpeT[:, j * 128:(j + 1) * 128], qp[:, s0:s0 + wd])],
                              reads=[knTT[tile_blk(j)], kpT[tile_blk(j)], qnT[bi], qpT[bi]])
                        pT = pTr.next()
                        kb.op(kb.ACT, lambda e: e.activation(out=pT[:, 0:wd], in_=sT[:, 0:wd], func=AF.Exp, scale=SC), reads=[sT], writes=[pT])
                        kb.mm(po, po[:, 0:wd], [(Vall[:, j, h * 128:(h + 1) * 128], pT[:, 0:wd])], reads=[VT[j], pT], start=(ji == 0), stop=(ji == len(keys) - 1))
                        kb.mm(pd, pd[:, 0:wd], [(self.ones[:], pT[:, 0:wd])], reads=[self.ones, pT], start=(ji == 0), stop=(ji == len(keys) - 1))
                    rd = rden.next()
                    o = oT.next()
                    kb.op(kb.DVE, lambda e: e.reciprocal(out=rd[:, 0:wd], in_=pd[:, 0:wd]), reads=[pd], writes=[rd])
                    kb.op(kb.DVE, lambda e: e.tensor_tensor(out=o[:, 0:wd], in0=po[:, 0:wd], in1=rd[:, 0:wd], op=ALU.mult), reads=[po, rd], writes=[o])
                    kb.dma(kb.POOL, "mixo", self.mixT[h].ap[:, s0:s0 + wd], o[:, 0:wd], reads=[o], writes=[self.mixT[h]])

    def phase_HG(self, l, b, xmodT, xmT, last):
        kb = self.kb
        W = self.W[l]
        I = self.I
        lbT = self.lb if l == 0 else self.lb1
        NCC = LC // CH
        SG8 = [(i, i + 8) for i in range(0, NCH, 8)]
        SG16 = [(0, 8)] + [(i, i + 16) for i in range(8, NCH, 16)]
        HV = 32

        def chunk_of(d, i):
            if d == 0:
                return i
            return NCC - 1 - i if i < NCC else NCH + NCC - 1 - i
        with kb.scope():
            WHr = Rot([kb.sbT(f"h_WH{i}", [128, 8, 512], BF16) for i in range(1)])
            whv = W["WH"].ap.rearrange("(kc p) n -> p kc n", p=128)
            mask = [kb.sbT("h_mf", [CH, CH], F32), kb.sbT("h_mb", [CH, CH], F32)]
            kb.dma(kb.SP, "w", mask[0][:], self.C["maskf"][:, :], writes=[mask[0]])
            kb.dma(kb.SP, "w", mask[1][:], self.C["maskb"][:, :], writes=[mask[1]])
            gcol = kb.sbT("h_gcol", [64, 4], F32)
            kb.dma(kb.SP, "w", gcol[:], I["hg_norm_g"][l].rearrange("(h v) -> v h", v=64), writes=[gcol], slow=True)
            one1 = kb.sbT("h_one", [128, 1], F32)
            kb.op(kb.DVE, lambda e: e.memset(one1[:], 1.0), writes=[one1])
            pr = self.prs(0, 8)
            Vc = kb.sb("h_Vc", [CH, NCH, 64], BF16)
            VcT = [T(Vc, f"Vc{g}") for g in range(NCH // 8)]
            zgT = kb.sbT("h_zgT", [64, NTOK], BF16)
            qS_t = kb.sb("h_qS", [128, NTOK], F32)
            bA_t = [kb.sb("h_bA0", [128, NTOK], F32), kb.sb("h_bA1", [128, NTOK], F32)]
            bB_t = kb.sb("h_bB", [128, NTOK], F32)
            bC_t = kb.sb("h_bC", [128, NTOK], F32)
            qS_T = [T(qS_t, f"qS{i}") for i in range(5)]
            bA_T = [[T(bA_t[d], f"bA{d}_{i}") for i in range(5)] for d in range(2)]
            bB_T = [T(bB_t, "bB")]
            bC_T = [T(bC_t, "bC")]
            qd = [kb.sbT(f"h_qd{d}", [128, NTOK], BF16) for d in range(2)]
            kd = [kb.sbT(f"h_kd{d}", [128, NTOK], BF16) for d in range(2)]
            ktm = kb.sbT("h_ktm", [CH, NCH, 128], BF16)
            oT = [kb.sbT(f"h_oT{d}", [64, NTOK], F32) for d in range(2)]
            ridx = kb.sbT("h_ridx", [128, NCH], F32)
            rslot = kb.sbT("h_rslot", [128, NCH], F32)
            dd = kb.sbT("h_dd", [128, NCH], F32)
            dmul = [kb.sbT(f"h_dmul{d}", [128, NCH], F32) for d in range(2)]
            d0t = [kb.sbT(f"h_d0{d}", [128, NCH], F32) for d in range(2)]
            Sbf = [kb.sbT(f"h_Sbf{d}", [128, NCH, 64], BF16) for d in range(2)]
            ATr = Rot([kb.sbT(f"h_AT{i}", [CH, 16, CH], BF16) for i in range(3)])
            rsr = Rot([kb.sbT(f"h_rs{i}", [64, 2, 512], F32) for i in range(1)])
            hgo = kb.sbT("h_hgo", [64, NTOK], BF16)
            pat = [kb.sbT("h_patlo", [128, CH], BF16), kb.sbT("h_pathi", [128, CH], BF16)]
            kb.op(kb.DVE, lambda e: e.memset(pat[0][:, 0:CH // 2], 1.0), writes=[pat[0]])
            kb.op(kb.DVE, lambda e: e.memset(pat[0][:, CH // 2:CH], 0.0), writes=[pat[0]])
            kb.op(kb.DVE, lambda e: e.memset(pat[1][:, 0:CH // 2], 0.0), writes=[pat[1]])
            kb.op(kb.DVE, lambda e: e.memset(pat[1][:, CH // 2:CH], 1.0), writes=[pat[1]])
            kA = kb.sbT("h_kA", [128, NTOK], BF16)
            kB = kb.sbT("h_kB", [128, NTOK], BF16)
            qB = kb.sbT("h_qB", [128, NTOK], BF16)
            for d in range(2):
                kb.op(kb.DVE, lambda e: e.memset(Sbf[d][:, 0, :], 0.0), writes=[Sbf[d]])
                kb.op(kb.DVE, lambda e: e.memset(dmul[d][:, NCH - 1:NCH], 1.0), writes=[dmul[d]])

            def v3(ap2d):
                return ap2d.rearrange("p (c t) -> p c t", t=CH)

            for h in range(4):
                WH = WHr.next()
                kb.dma(kb.SP, "w", WH[:], whv[:, :, h * 512:(h + 1) * 512], reads=[W["WH"]], writes=[WH])
                for g in range(NCH // 8):
                    pt = pr.next()
                    kb.mm_multi(pt, [(pt[0:CH, j * 64:(j + 1) * 64], [(xmodT[:, k, (g * 8 + j) * CH:(g * 8 + j + 1) * CH], WH[:, k, 448:512]) for k in range(8)])
                                     for j in range(8)], [WH, xmT[2 * g], xmT[2 * g + 1]])
                    self.copy(kb.DVE, Vc[:, g * 8:(g + 1) * 8, :], pt[0:CH, :].rearrange("p (j v) -> p j v", v=64), [pt], [VcT[g]])
                if getattr(self, 'hg_stop', 99) <= 1:
                    return
                for bi, (s0, wd) in enumerate(TBLK):
                    tl = [0, 1] if bi == 0 else list(range(2 + 4 * (bi - 1), 6 + 4 * (bi - 1)))
                    xr = [xmT[i] for i in tl]
                    pq, pf, pb, pg = pr.next(), pr.next(), pr.next(), pr.next()
                    for pt, c0, m in ((pq, 0, 128), (pf, 128, 128), (pb, 256, 128), (pg, 384, 64)):
                        kb.mm(pt, pt[0:m, 0:wd], [(WH[:, k, c0:c0 + m], xmodT[:, k, s0:s0 + wd]) for k in range(8)], reads=[WH] + xr)
                    kb.op(kb.ACT, lambda e: e.activation(out=qS_t[:, s0:s0 + wd], in_=pq[:, 0:wd], func=AF.Silu), reads=[pq], writes=[qS_T[bi]])
                    kb.op(kb.ACT, lambda e: e.activation(out=zgT[:, s0:s0 + wd], in_=pg[0:64, 0:wd], func=AF.Silu), reads=[pg], writes=[zgT])
                    kb.op(kb.ACT, lambda e: e.activation(out=bA_t[0][:, s0:s0 + wd], in_=pf[:, 0:wd], func=AF.Sigmoid), reads=[pf], writes=[bA_T[0][bi]])
                    kb.op(kb.ACT, lambda e: e.activation(out=bA_t[1][:, s0:s0 + wd], in_=pb[:, 0:wd], func=AF.Sigmoid), reads=[pb], writes=[bA_T[1][bi]])
                if getattr(self, 'hg_stop', 99) <= 2:
                    return
                for d in range(2):
                    bA = bA_t[d]
                    idx = d * 4 + h
                    lbc = lbT[:, 0, idx:idx + 1]
                    omc = lbT[:, 1, idx:idx + 1]
                    kb.op(kb.DVE, lambda e: e.tensor_scalar(out=bA[:], in0=bA[:], scalar1=omc, scalar2=lbc, op0=ALU.mult, op1=ALU.add),
                          reads=bA_T[d] + [lbT], writes=bA_T[d])
                    kb.op(kb.ACT, lambda e: e.activation(out=bB_t[:], in_=bA[:], func=AF.Ln), reads=bA_T[d], writes=bB_T)
                    kb.op(kb.POOL, lambda e: e.tensor_scalar(out=bA[:], in0=bA[:], scalar1=-1.0, scalar2=1.0, op0=ALU.mult, op1=ALU.add),
                          reads=bA_T[d], writes=bA_T[d])
                    ob = one1[:, 0:1]
                    if d == 0:
                        kb.op(kb.DVE, lambda e: e.tensor_tensor_scan(out=bC_t[:], data0=ob.to_broadcast([128, NTOK]), data1=bB_t[:], initial=0.0,
                                                                     op0=ALU.mult, op1=ALU.add), reads=bB_T + [one1], writes=bC_T)
                        mid = CH // 2 - 1
                    else:
                        kb.op(kb.DVE, lambda e: e.tensor_tensor_scan(out=bC_t[:, 0:LC][:, ::-1], data0=ob.to_broadcast([128, LC]), data1=bB_t[:, 0:LC][:, ::-1],
                                                                     initial=0.0, op0=ALU.mult, op1=ALU.add), reads=bB_T + [one1], writes=bC_T)
                        kb.op(kb.DVE, lambda e: e.tensor_tensor_scan(out=bC_t[:, LC:NTOK][:, ::-1], data0=ob.to_broadcast([128, SEQ]), data1=bB_t[:, LC:NTOK][:, ::-1],
                                                                     initial=bC_t[:, 0:1], op0=ALU.mult, op1=ALU.add), reads=bB_T + bC_T + [one1], writes=bC_T)
                        mid = CH // 2
                    c3 = v3(bC_t[:])
                    kb.op(kb.DVE, lambda e: e.tensor_copy(out=ridx[:], in_=c3[:, :, mid]), reads=bC_T, writes=[ridx])
                    if d == 0:
                        kb.op(kb.DVE, lambda e: e.tensor_copy(out=rslot[:], in_=ridx[:]), reads=[ridx], writes=[rslot])
                    else:
                        kb.op(kb.DVE, lambda e: e.tensor_copy(out=rslot[:, 0:NCC], in_=ridx[:, 0:NCC][:, ::-1]), reads=[ridx], writes=[rslot])
                        kb.op(kb.DVE, lambda e: e.tensor_copy(out=rslot[:, NCC:NCH], in_=ridx[:, NCC:NCH][:, ::-1]), reads=[ridx], writes=[rslot])
                    kb.op(kb.DVE, lambda e: e.tensor_tensor(out=dd[:, 0:NCH - 1], in0=rslot[:, 1:NCH], in1=rslot[:, 0:NCH - 1], op=ALU.subtract),
                          reads=[rslot], writes=[dd])
                    kb.op(kb.ACT, lambda e: e.activation(out=dmul[d][:, 0:NCH - 1], in_=dd[:, 0:NCH - 1], func=AF.Exp), reads=[dd], writes=[dmul[d]])
                    kb.op(kb.DVE, lambda e: e.tensor_copy(out=d0t[d][:], in_=dmul[d][:]), reads=[dmul[d]], writes=[d0t[d]])
                    kb.op(kb.DVE, lambda e: e.memset(d0t[d][:, 0:1], 0.0), writes=[d0t[d]])
                    kb.op(kb.DVE, lambda e: e.tensor_tensor(out=c3, in0=c3, in1=ridx[:].unsqueeze(2).to_broadcast([128, NCH, CH]), op=ALU.subtract),
                          reads=bC_T + [ridx], writes=bC_T)
                    kb.op(kb.ACT, lambda e: e.activation(out=bB_t[:], in_=bC_t[:], func=AF.Exp), reads=bC_T, writes=bB_T)
                    kb.op(kb.DVE, lambda e: e.tensor_tensor(out=qd[d][:], in0=qS_t[:], in1=bB_t[:], op=ALU.mult), reads=qS_T + bB_T, writes=[qd[d]])
                    kb.op(kb.ACT, lambda e: e.activation(out=bC_t[:], in_=bC_t[:], func=AF.Exp, scale=-1.0), reads=bC_T, writes=bC_T)
                    kb.op(kb.DVE, lambda e: e.tensor_tensor(out=kd[d][:], in0=bA[:], in1=bC_t[:], op=ALU.mult), reads=bA_T[d] + bC_T, writes=[kd[d]])
                if getattr(self, 'hg_stop', 99) <= 3:
                    return
                for d in range(2):
                    data0m, data1, SS = (bC_t, bA_t[0], bB_t) if d == 0 else (bC_t, bA_t[1], qS_t)
                    data0T, data1T, SST = (bC_T, bA_T[0], bB_T) if d == 0 else (bC_T, bA_T[1], qS_T)
                    for (g0, g1) in SG8:
                        pt = pr.next()
                        pv = self.psbf(pt)
                        kb.tr(pt, [(pv[0:CH, j * 128:(j + 1) * 128], kd[d][:, (g0 + j) * CH:(g0 + j + 1) * CH]) for j in range(8)], self.ident, reads=[kd[d]])
                        self.copy(kb.ACT, ktm[:, g0:g1, :], pv[0:CH, :].rearrange("p (j k) -> p j k", k=128), [pt], [ktm])
                    if getattr(self, 'hg_stop', 99) <= 4:
                        return
                    kb.op(kb.DVE, lambda e: e.tensor_copy(out=data0m[:].rearrange("p (v i) -> p v i", i=NCH), in_=d0t[d][:].unsqueeze(1).to_broadcast([128, HV, NCH])),
                          reads=[d0t[d]], writes=data0T)
                    for vh in range(2):
                        d1v = data1[:].rearrange("p (v i) -> p i v", i=NCH)
                        for (i0, i1) in SG8:
                            pt = pr.next()
                            grp = []
                            for j in range(8):
                                c = chunk_of(d, i0 + j)
                                grp.append((pt[:, j * 64:(j + 1) * 64], [(ktm[:, c, :], Vc[:, c, :])]))
                            kb.mm_multi(pt, grp, [ktm] + VcT)
                            kb.op(kb.DVE, lambda e: e.tensor_tensor(out=d1v[:, i0:i1, :], in0=pt[:, :].rearrange("p (j v) -> p j v", v=64)[:, :, vh * HV:(vh + 1) * HV],
                                                                    in1=dmul[d][:, i0:i1].unsqueeze(2).to_broadcast([128, 8, HV]), op=ALU.mult),
                                  reads=[pt, dmul[d]], writes=data1T)
                        kb.op(kb.DVE, lambda e: e.tensor_tensor_scan(out=SS[:], data0=data0m[:], data1=data1[:], initial=0.0, op0=ALU.mult, op1=ALU.add),
                              reads=data0T + data1T, writes=SST)
                        kb.op(kb.DVE, lambda e: e.tensor_copy(out=Sbf[d][:, 1:NCH, vh * HV:(vh + 1) * HV], in_=SS[:].rearrange("p (v i) -> p i v", i=NCH)[:, 0:NCH - 1, :]),
                              reads=SST, writes=[Sbf[d]])
                    if getattr(self, 'hg_stop', 99) <= 5:
                        return
                    pA, pB = (pat[0], pat[1]) if d == 0 else (pat[1], pat[0])
                    for dst, src, pp in ((kA, kd[d], pA), (kB, kd[d], pB), (qB, qd[d], pB)):
                        kb.op(kb.POOL, lambda e: e.tensor_tensor(out=v3(dst[:]), in0=v3(src[:]), in1=pp[:].unsqueeze(1).to_broadcast([128, NCH, CH]), op=ALU.mult),
                              reads=[src, pp], writes=[dst])
                    for (i0, i1) in SG16:
                        n = i1 - i0
                        cs = [chunk_of(d, i0 + j) for j in range(n)]
                        pa = pr.next()
                        kb.mm_multi(pa, [(pa[0:CH, j * CH:(j + 1) * CH], [(kA[:, c * CH:(c + 1) * CH], qd[d][:, c * CH:(c + 1) * CH]),
                                                                         (kB[:, c * CH:(c + 1) * CH], qB[:, c * CH:(c + 1) * CH])]) for j, c in enumerate(cs)],
                                    [kA, kB, qB, qd[d]])
                        AT = ATr.next()
                        kb.op(kb.DVE, lambda e: e.tensor_tensor(out=AT[:, 0:n, :], in0=pa[0:CH, 0:n * CH].rearrange("p (j t) -> p j t", t=CH),
                                                                in1=mask[d][:].unsqueeze(1).to_broadcast([CH, n, CH]), op=ALU.mult),
                              reads=[pa, mask[d]], writes=[AT])
                        po = pr.next()
                        grp = []
                        for j, c in enumerate(cs):
                            grp.append((po[0:64, j * CH:(j + 1) * CH], [(Vc[:, c, :], AT[:, j, :]), (Sbf[d][:, i0 + j, :], qd[d][:, c * CH:(c + 1) * CH])]))
                        kb.mm_multi(po, grp, [AT, qd[d], Sbf[d]] + VcT)
                        clo, chi = min(cs), max(cs)
                        ov = oT[d][:, clo * CH:(chi + 1) * CH].rearrange("p (j t) -> p j t", t=CH)
                        if d == 1:
                            ov = ov[:, ::-1, :]
                        self.copy(kb.ACT, ov, po[0:64, 0:n * CH].rearrange("p (j t) -> p j t", t=CH), [po], [oT[d]])
                if getattr(self, 'hg_stop', 99) <= 6:
                    return
                kb.op(kb.DVE, lambda e: e.tensor_tensor(out=oT[0][:], in0=oT[0][:], in1=oT[1][:], op=ALU.add), reads=[oT[0], oT[1]], writes=[oT[0]])
                sqb = bB_t[0:64, :].bitcast(BF16)
                kb.op(kb.ACT, lambda e: e.activation(out=sqb[:, 0:NTOK], in_=oT[0][:], func=AF.Square), reads=[oT[0]], writes=bB_T)
                for bi, (s0, wd) in enumerate(TBLK):
                    pt = pr.next()
                    kb.mm(pt, pt[0:64, 0:wd], [(self.ones[0:64, 0:64], sqb[:, s0:s0 + wd])], reads=[self.ones] + bB_T)
                    rs = rsr.next()
                    self.rstd(rs[:, 0, 0:wd], pt[0:64, 0:wd], rs[:, 1, 0:wd], 1.0 / 64, [rs], [pt])
                    kb.op(kb.DVE, lambda e: e.scalar_tensor_tensor(out=rs[:, 1, 0:wd], in0=oT[0][:, s0:s0 + wd], scalar=gcol[:, h:h + 1], in1=rs[:, 0, 0:wd],
                                                                   op0=ALU.mult, op1=ALU.mult), reads=[oT[0], gcol, rs], writes=[rs])
                    kb.op(kb.DVE, lambda e: e.tensor_tensor(out=hgo[:, s0:s0 + wd], in0=rs[:, 1, 0:wd], in1=zgT[:, s0:s0 + wd], op=ALU.mult),
                          reads=[rs, zgT], writes=[hgo])
                if getattr(self, 'hg_stop', 99) <= 7:
                    return
                mt = self.mixT[4 + h // 2]
                kb.dma(kb.POOL, "mixo", mt.ap[(h % 2) * 64:(h % 2 + 1) * 64, :], hgo[:], reads=[hgo], writes=[mt])
                if getattr(self, 'hg_stop', 99) <= 8 + h:
                    return

    def phase_FN(self, l, b, xmodT, xmT, last):
        kb = self.kb
        W = self.W[l]
        with kb.scope():
            WF = kb.sbT("f_WF", [128, 8, 256], BF16)
            BD = kb.sbT("f_BD", [128, 2, 512], BF16)
            WFO = kb.sbT("f_WFO", [128, 2, 256], BF16)
            self.wload(WF, W["WF"], 8)
            self.wload(WFO, W["WFO"], 2)
            kb.dma(kb.SP, "w", BD[:], self.C["bd"].rearrange("(kc p) n -> p kc n", p=128), writes=[BD])
            tcc = kb.sbT("f_tcc", [128, 2, LC], BF16)
            tsc = kb.sbT("f_tsc", [128, 2, LC], BF16)
            kb.dma(kb.SP, "w", tcc[:], self.C["dftc_c"].rearrange("(kc p) n -> p kc n", p=128), writes=[tcc])
            kb.dma(kb.SP, "w", tsc[:], self.C["dfts_c"].rearrange("(kc p) n -> p kc n", p=128), writes=[tsc])
            zT = kb.sb("f_zT", [128, 2, NTOK], BF16)
            zTT = [T(zT, f"zT{i}") for i in range(5)]
            zcs = kb.sb("f_zcs", [128, NT, 512], BF16)
            zcT = [T(zcs, f"zc{i}") for i in range(NT)]
            mx = kb.sb("f_mx", [128, 2, NTOK], BF16)
            mxT = [T(mx, f"mx{i}") for i in range(5)]
            tabC = Rot([kb.sbT(f"f_tC{i}", [128, 16, 512], BF16) for i in range(3)])
            tabS = Rot([kb.sbT(f"f_tS{i}", [128, 16, 512], BF16) for i in range(3)])
            fo = Rot([kb.sbT(f"f_fo{i}", [128, 512], BF16) for i in range(2)])
            pr = self.prs(0, 8)
            for bi, (s0, wd) in enumerate(TBLK):
                tl = [0, 1] if bi == 0 else list(range(2 + 4 * (bi - 1), 6 + 4 * (bi - 1)))
                for m in range(2):
                    pt = pr.next()
                    kb.mm(pt, pt[:, 0:wd], [(WF[:, k, m * 128:(m + 1) * 128], xmodT[:, k, s0:s0 + wd]) for k in range(8)], reads=[WF] + [xmT[i] for i in tl])
                    self.copy(kb.ACT, zT[:, m, s0:s0 + wd], pt[:, 0:wd], [pt], [zTT[bi]])
            for i in range(NT):
                bi = 0 if i < 2 else 1 + (i - 2) // 4
                pt = pr.next()
                kb.mm(pt, pt[:, :], [(zT[:, m, i * 128:(i + 1) * 128], BD[:, m, :]) for m in range(2)], reads=[BD, zTT[bi]])
                self.copy(kb.DVE, zcs[:, i, :], pt[:, :], [pt], [zcT[i]])
            for m in range(2):
                pt = pr.next()
                prs_ = [(zcs[:, tt, m * 128:(m + 1) * 128], tcc[:, tt, :]) for tt in range(2)] + [(zcs[:, tt, 256 + m * 128:256 + (m + 1) * 128], tsc[:, tt, :]) for tt in range(2)]
                kb.mm(pt, pt[:, 0:LC], prs_, reads=[tcc, tsc, zcT[0], zcT[1]])
                self.copy(kb.ACT, mx[:, m, 0:LC], pt[:, 0:LC], [pt], [mxT[0]])
            cv = self.C["dftc"].rearrange("(tt p) k -> p tt k", p=128)
            sv = self.C["dfts"].rearrange("(tt p) k -> p tt k", p=128)
            for k4 in range(4):
                tc_, ts_ = tabC.next(), tabS.next()
                kb.dma(kb.SP, "tab", tc_[:], cv[:, :, k4 * 512:(k4 + 1) * 512], writes=[tc_])
                kb.dma(kb.SP, "tab", ts_[:], sv[:, :, k4 * 512:(k4 + 1) * 512], writes=[ts_])
                for m in range(2):
                    pt = pr.next()
                    prs_ = [(zcs[:, 2 + tt, m * 128:(m + 1) * 128], tc_[:, tt, :]) for tt in range(16)] + \
                           [(zcs[:, 2 + tt, 256 + m * 128:256 + (m + 1) * 128], ts_[:, tt, :]) for tt in range(16)]
                    kb.mm(pt, pt[:, :], prs_, reads=[tc_, ts_] + zcT[2:])
                    self.copy(kb.ACT, mx[:, m, LC + k4 * 512:LC + (k4 + 1) * 512], pt[:, :], [pt], [mxT[1 + k4]])
            for bi, (s0, wd) in enumerate(TBLK):
                for m2 in range(2):
                    pt = pr.next()
                    kb.mm(pt, pt[:, 0:wd], [(WFO[:, m, m2 * 128:(m2 + 1) * 128], mx[:, m, s0:s0 + wd]) for m in range(2)], reads=[WFO, mxT[bi]])
                    f = fo.next()
                    self.copy(kb.DVE, f[:, 0:wd], pt[:, 0:wd], [pt], [f])
                    kb.dma(kb.POOL, "mixo", self.mixT[6 + m2].ap[:, s0:s0 + wd], f[:, 0:wd], reads=[f], writes=[self.mixT[6 + m2]])

    def phase_EG(self, l, b, last):
        kb = self.kb
        W = self.W[l]
        with kb.scope():
            WO = kb.sbT("e_WO", [128, 8, D], BF16)
            self.wload(WO, W["WO"], 8)
            bc = {}
            for nm, m, slot in [("G1l", b, 2), ("S2l", b, 4), ("H2l", b, 3), ("G2l", b, 5)] + ([] if last else [("G1c", 4, 2), ("S2c", 4, 4), ("H2c", 4, 3), ("G2c", 4, 5)]):
                bc[nm] = kb.sbT("e_" + nm, [128, D], F32)
                self.load_bc(bc[nm], l, m, slot)
            if last:
                gF = kb.sbT("e_gF", [128, D], F32)
                kb.dma(kb.SP, "w", gF[:], self.I["final_norm_g"].rearrange("(o n) -> o n", o=1).partition_broadcast(128), writes=[gF])
            r = self.norm_res("e_")
            xin = Rot([kb.sbT(f"e_xin{i}", [128, D], F32) for i in range(2)])
            ytmp = Rot([kb.sbT(f"e_yt{i}", [128, 512], F32) for i in range(2)])
            sgr = Rot([kb.sbT(f"e_sg{i}", [128, 512], F32) for i in range(2)])
            x1 = kb.sb("e_x1", [128, 6, D], F32)
            x1T = [T(x1, f"x1_{i}") for i in range(6)]
            xm2 = kb.sb("e_xm2", [128, 8, 768], BF16)
            xm2T = [T(xm2, f"xm2_{i}") for i in range(6)]
            actT = kb.sb("e_act", [128, NF, 768], BF16)
            actTT = [T(actT, f"act{i}") for i in range(NF)]
            mixb = kb.sbT("e_mixb", [128, 8, 768], BF16)
            wgu = Rot([kb.sbT(f"e_wgu{i}", [128, 8, 512], BF16) for i in range(2)])
            wdn = Rot([kb.sbT(f"e_wdn{i}", [128, NF, 256], BF16) for i in range(2)])
            if last:
                fin = Rot([kb.sbT(f"e_fin{i}", [128, D], F32) for i in range(2)])
                fst = Rot([kb.sbT(f"e_fst{i}", [128, 4], F32) for i in range(2)])
            guv = W["WGU"].ap.rearrange("(kc p) n -> p kc n", p=128)
            dnv = W["WDN"].ap.rearrange("(kc p) n -> p kc n", p=128)
            pr = self.prs(0, 8)
            for blk in range(3):
                tiles = [i for i in range(6 * blk, 6 * blk + 6) if not (last and i < 2)]
                nt_ = len(tiles)
                t0 = tiles[0] * 128
                ntk = nt_ * 128
                subs = [(o, min(512, ntk - o)) for o in range(0, ntk, 512)]
                for c in range(8):
                    kb.dma(kb.SP, "mixi", mixb[:, c, 0:ntk], self.mixT[c].ap[:, t0:t0 + ntk], reads=[self.mixT[c]], writes=[mixb])
                for ti, i in enumerate(tiles):
                    xt = xin.next()
                    src, srcT = self.x_src(l, b, i)
                    kb.dma(kb.SP, "xin", xt[:], src, reads=[srcT] if srcT else [], writes=[xt])
                    G1 = bc["G1c"] if i < 2 else bc["G1l"]
                    for hf in range(2):
                        pt = pr.next()
                        kb.mm(pt, pt[:, :], [(mixb[:, c, ti * 128:(ti + 1) * 128], WO[:, c, hf * 512:(hf + 1) * 512]) for c in range(8)], reads=[mixb, WO])
                        yt = ytmp.next()
                        kb.op(kb.DVE, lambda e: e.tensor_tensor(out=yt[:], in0=pt[:, :], in1=G1[:, hf * 512:(hf + 1) * 512], op=ALU.mult), reads=[pt, G1], writes=[yt])
                        kb.op(kb.DVE, lambda e: e.tensor_tensor(out=x1[:, ti, hf * 512:(hf + 1) * 512], in0=xt[:, hf * 512:(hf + 1) * 512], in1=yt[:], op=ALU.add),
                              reads=[xt, yt], writes=[x1T[ti]])
                for ti, i in enumerate(tiles):
                    S2, H2 = (bc["S2c"], bc["H2c"]) if i < 2 else (bc["S2l"], bc["H2l"])
                    self.norm_tile(r, T_view(x1T[ti], x1[:, ti, :]), S2, H2, xm2T[ti], xm2[:, :, ti * 128:(ti + 1) * 128])
                for fg in range(NF // 2):
                    w = wgu.next()
                    kb.dma(kb.SP, "wg", w[:, :, 0:256], guv[:, :, fg * 256:(fg + 1) * 256], reads=[W["WGU"]], writes=[w])
                    kb.dma(kb.SP, "wg", w[:, :, 256:512], guv[:, :, FFH + fg * 256:FFH + (fg + 1) * 256], reads=[W["WGU"]], writes=[w])
                    for fc in range(2):
                        f = fg * 2 + fc
                        for (o, wd) in subs:
                            xr = xm2T[o // 128:(o + wd) // 128]
                            pg, pu = pr.next(), pr.next()
                            kb.mm(pg, pg[:, 0:wd], [(w[:, k, fc * 128:(fc + 1) * 128], xm2[:, k, o:o + wd]) for k in range(8)], reads=[w] + xr)
                            kb.mm(pu, pu[:, 0:wd], [(w[:, k, 256 + fc * 128:256 + (fc + 1) * 128], xm2[:, k, o:o + wd]) for k in range(8)], reads=[w] + xr)
                            sg = sgr.next()
                            kb.op(kb.ACT, lambda e: e.activation(out=sg[:, 0:wd], in_=pg[:, 0:wd], func=AF.Silu), reads=[pg], writes=[sg])
                            kb.op(kb.DVE, lambda e: e.tensor_tensor(out=actT[:, f, o:o + wd], in0=pu[:, 0:wd], in1=sg[:, 0:wd], op=ALU.mult), reads=[pu, sg], writes=[actTT[f]])
                for q4 in range(4):
                    w = wdn.next()
                    cs_ = slice(q4 * 256, (q4 + 1) * 256)
                    kb.dma(kb.SP, "wd", w[:], dnv[:, :, cs_], reads=[W["WDN"]], writes=[w])
                    for ti, i in enumerate(tiles):
                        G2 = bc["G2c"] if i < 2 else bc["G2l"]
                        pt = pr.next()
                        kb.mm(pt, pt[:, 0:256], [(actT[:, f, ti * 128:(ti + 1) * 128], w[:, f, :]) for f in range(NF)], reads=[w] + actTT)
                        yt = ytmp.next()
                        kb.op(kb.DVE, lambda e: e.tensor_tensor(out=yt[:, 0:256], in0=pt[:, 0:256], in1=G2[:, cs_], op=ALU.mult), reads=[pt, G2], writes=[yt])
                        kb.op(kb.DVE, lambda e: e.tensor_tensor(out=x1[:, ti, cs_], in0=x1[:, ti, cs_], in1=yt[:, 0:256], op=ALU.add),
                              reads=[x1T[ti], yt], writes=[x1T[ti]])
                for ti, i in enumerate(tiles):
                    if not last:
                        kb.dma(kb.POOL, "xso", self.xs[b].ap[i * 128:(i + 1) * 128, :], x1[:, ti, :], reads=[x1T[ti]], writes=[self.xs[b]])
                    else:
                        st = fst.next()
                        fo_ = fin.next()
                        junk = r["junk"]
                        kb.op(kb.ACT, lambda e: e.activation(out=junk[:], in_=x1[:, ti, :], func=AF.Square, accum_out=st[:, 0:1]), reads=[x1T[ti]], writes=[junk, st])
                        self.rstd(st[:, 2:3], st[:, 0:1], st[:, 1:2], 1.0 / D, [st])
                        kb.op(kb.DVE, lambda e: e.scalar_tensor_tensor(out=fo_[:], in0=x1[:, ti, :], scalar=st[:, 2:3], in1=gF[:], op0=ALU.mult, op1=ALU.mult),
                              reads=[x1T[ti], st, gF], writes=[fo_])
                        kb.dma(kb.POOL, "out", self.out[b, (i - 2) * 128:(i - 1) * 128, :], fo_[:], reads=[fo_], writes=[self.outT])

    def build(self):
        kb = self.kb
        self.convert_weights()
        self.phase_cact()
        for l in self.layers:
            self.phase_mod(l)
        for b in range(self.NB):
            for l in self.layers:
                last = (l == DEPTH - 1)
                with kb.scope():
                    xmodT = kb.sb("xmodT", [128, 8, NTOK], BF16)
                    xmT = [T(xmodT, f"xm{i}") for i in range(NT)]
                    self.phase_A(l, b, xmodT, xmT)
                    self.phase_MLA(l, b, xmodT, xmT, last)
                    self.phase_HG(l, b, xmodT, xmT, last)
                    self.phase_FN(l, b, xmodT, xmT, last)
                self.phase_EG(l, b, last)
        kb.finish()
        return kb.nc


class T_view:
    def __init__(self, base, ap):
        self.base = base
        self.ap = ap

    def __getitem__(self, k):
        return self.ap[k]

    w = property(lambda s: s.base.w, lambda s, v: setattr(s.base, "w", v))
    r = property(lambda s: s.base.r, lambda s, v: setattr(s.base, "r", v))
    excl = property(lambda s: s.base.excl)


_CACHE = {}
NB_PER_LAUNCH = 4


def kernel(**inputs):
    NBC = BATCH // NCORES
    NB = NB_PER_LAUNCH
    if "nc" not in _CACHE:
        _CACHE["nc"] = Prog(NB=NB, layers=(0, 1)).build()
        _CACHE["consts"] = make_consts()
    nc = _CACHE["nc"]
    consts = _CACHE["consts"]
    full = {name: np.asarray(inputs[name], dtype=np.float32) for name, _ in INPUT_SPECS}
    out = np.empty((BATCH, SEQ, D), dtype=np.float32)
    for j in range(0, NBC, NB):
        in_maps = []
        for i in range(NCORES):
            m = {}
            b0 = i * NBC + j
            for name, shape in INPUT_SPECS:
                a = full[name]
                if shape[0] is None:
                    a = a[b0:b0 + NB]
                m[name] = np.ascontiguousarray(a)
            for k, v in consts.items():
                m["k_" + k] = v
            in_maps.append(m)
        res = run_bass_kernel_spmd(nc, in_maps, core_ids=list(range(NCORES)))
        for i in range(NCORES):
            b0 = i * NBC + j
            out[b0:b0 + NB] = np.asarray(res.results[i]["out"], dtype=np.float32)
    return out
```
